# Optimizing a Trainium2 kernel written in Bass

```python
import math
import jax, jax.numpy as jnp
from jax import lax
import numpy as np

D_MODEL = 4096
BATCH = 1
SEQ = 8192
DEPTH = 1

ATTN_HEADS = 8
ATTN_HEAD_DIM = 128
ATTN_V_DIM = 2 * ATTN_HEAD_DIM
ATTN_WIDTH = ATTN_HEADS * ATTN_V_DIM
Q_COLS = ATTN_HEADS * 2 * ATTN_HEAD_DIM
K_COLS = ATTN_HEADS * 2 * ATTN_HEAD_DIM
Q_BLOCK = 128
ROPE_THETA = 500000.0
ROPE_DIM = ATTN_HEAD_DIM // 4

SGU_WIDTH = D_MODEL // 2
SGU_GROUPS = 8
SGU_GROUP_DIM = SGU_WIDTH // SGU_GROUPS
CHUNK = 128

D_FF = 4 * D_MODEL

ALPHA = (2.0 * DEPTH) ** 0.25
BETA = (8.0 * DEPTH) ** -0.25
LN_EPS = 1e-5

IN_WIDTHS = (Q_COLS, K_COLS, ATTN_WIDTH, SGU_WIDTH, SGU_WIDTH, D_MODEL, D_MODEL)
IN_COLS = sum(IN_WIDTHS)
SPLIT_POINTS = tuple(int(c) for c in np.cumsum(IN_WIDTHS)[:-1])

kernel_name = "hybrid_diffattn_chunked_sgu_gated_deepnorm"


def layer_norm(x, g, b):
    xf = x.astype(jnp.float32)
    mu = jnp.mean(xf, axis=-1, keepdims=True)
    var = jnp.mean(jnp.square(xf - mu), axis=-1, keepdims=True)
    y = (xf - mu) * lax.rsqrt(var + LN_EPS) * g.astype(jnp.float32) + b.astype(jnp.float32)
    return y.astype(x.dtype)


def partial_rope(t, pos):
    half = ROPE_DIM // 2
    inv_freq = ROPE_THETA ** (-jnp.arange(0, ROPE_DIM, 2, dtype=jnp.float32) / ROPE_DIM)
    ang = pos.astype(jnp.float32)[:, None] * inv_freq[None, :]
    cos = jnp.cos(ang)[None, :, None, None, :]
    sin = jnp.sin(ang)[None, :, None, None, :]
    rot = t[..., :ROPE_DIM].astype(jnp.float32)
    r1, r2 = rot[..., :half], rot[..., half:]
    rotated = jnp.concatenate([r1 * cos - r2 * sin, r2 * cos + r1 * sin], axis=-1)
    return jnp.concatenate([rotated.astype(t.dtype), t[..., ROPE_DIM:]], axis=-1)


def diff_attention(q, k, v, lam, subln_w, lambda_init):
    B, S = q.shape[0], q.shape[1]
    qh = q.transpose(0, 2, 3, 1, 4)
    kh = k.transpose(0, 2, 3, 1, 4)
    vh = v.transpose(0, 2, 1, 3)
    scale = ATTN_HEAD_DIM ** -0.5
    k_pos = jnp.arange(S)

    def one_block(start):
        qb = lax.dynamic_slice_in_dim(qh, start, Q_BLOCK, axis=3)
        s = jnp.einsum('bhcqd,bhckd->bhcqk', qb, kh).astype(jnp.float32) * scale
        q_pos = start + jnp.arange(Q_BLOCK)
        causal = q_pos[:, None] >= k_pos[None, :]
        s = jnp.where(causal, s, -jnp.inf)
        p = jax.nn.softmax(s, axis=-1)
        a = p[:, :, 0] - lam * p[:, :, 1]
        return jnp.einsum('bhqk,bhkd->bhqd', a.astype(vh.dtype), vh)

    starts = jnp.arange(S // Q_BLOCK) * Q_BLOCK
    o = lax.map(one_block, starts)
    o = o.transpose(1, 0, 3, 2, 4).reshape(B, S, ATTN_HEADS, ATTN_V_DIM)
    of = o.astype(jnp.float32)
    of = of * lax.rsqrt(jnp.mean(jnp.square(of), axis=-1, keepdims=True) + LN_EPS)
    of = of * subln_w.astype(jnp.float32) * (1.0 - lambda_init)
    return of.reshape(B, S, ATTN_WIDTH).astype(v.dtype)


def chunked_sgu(u, s, ln_g, ln_b, w_s, b_s):
    B, S, _ = s.shape
    s = layer_norm(s, ln_g, ln_b)
    sc = s.reshape(B, S // CHUNK, CHUNK, SGU_GROUPS, SGU_GROUP_DIM)
    causal = jnp.tril(jnp.ones((CHUNK, CHUNK), dtype=bool))
    w = jnp.where(causal[None], w_s, 0.0)
    mixed = jnp.einsum('gts,bcsgd->bctgd', w, sc) + b_s.T[None, None, :, :, None]
    return u * mixed.reshape(B, S, SGU_WIDTH)


def setup_inputs(seed: int = 0) -> dict:
    key = jax.random.key(seed)
    ks = jax.random.split(key, 24)
    f32 = jnp.float32
    nrm = lambda k, shape: jax.random.normal(k, shape, dtype=f32)
    L = DEPTH
    return {
        "x": nrm(ks[0], (BATCH, SEQ, D_MODEL)),
        "w_in": nrm(ks[1], (L, D_MODEL, IN_COLS)) * D_MODEL ** -0.5,
        "lambda_q1": nrm(ks[2], (L, ATTN_HEAD_DIM)) * 0.1,
        "lambda_k1": nrm(ks[3], (L, ATTN_HEAD_DIM)) * 0.1,
        "lambda_q2": nrm(ks[4], (L, ATTN_HEAD_DIM)) * 0.1,
        "lambda_k2": nrm(ks[5], (L, ATTN_HEAD_DIM)) * 0.1,
        "subln_w": 1.0 + 0.02 * nrm(ks[6], (L, ATTN_V_DIM)),
        "sgu_ln_g": 1.0 + 0.02 * nrm(ks[7], (L, SGU_WIDTH)),
        "sgu_ln_b": 0.02 * nrm(ks[8], (L, SGU_WIDTH)),
        "w_spatial": nrm(ks[9], (L, SGU_GROUPS, CHUNK, CHUNK)) * CHUNK ** -0.5,
        "b_spatial": 1.0 + 0.02 * nrm(ks[10], (L, SGU_GROUPS, CHUNK)),
        "w_proj_attn": nrm(ks[11], (L, ATTN_WIDTH, D_MODEL)) * ATTN_WIDTH ** -0.5,
        "w_proj_sgu": nrm(ks[12], (L, SGU_WIDTH, D_MODEL)) * SGU_WIDTH ** -0.5,
        "w_out": nrm(ks[13], (L, D_MODEL, D_MODEL)) * (D_MODEL ** -0.5 * BETA),
        "ln1_g": 1.0 + 0.02 * nrm(ks[14], (L, D_MODEL)),
        "ln1_b": 0.02 * nrm(ks[15], (L, D_MODEL)),
        "w_mlp_in": nrm(ks[16], (L, D_MODEL, D_FF)) * D_MODEL ** -0.5,
        "b_mlp_in": 0.02 * nrm(ks[17], (L, D_FF)),
        "w_mlp_out": nrm(ks[18], (L, D_FF, D_MODEL)) * (D_FF ** -0.5 * BETA),
        "b_mlp_out": 0.02 * nrm(ks[19], (L, D_MODEL)),
        "ln2_g": 1.0 + 0.02 * nrm(ks[20], (L, D_MODEL)),
        "ln2_b": 0.02 * nrm(ks[21], (L, D_MODEL)),
    }


def reference(x, w_in, lambda_q1, lambda_k1, lambda_q2, lambda_k2, subln_w,
              sgu_ln_g, sgu_ln_b, w_spatial, b_spatial, w_proj_attn, w_proj_sgu,
              w_out, ln1_g, ln1_b, w_mlp_in, b_mlp_in, w_mlp_out, b_mlp_out,
              ln2_g, ln2_b):
    B, S, _ = x.shape
    pos = jnp.arange(S)
    h = x
    for l in range(DEPTH):
        lambda_init = 0.8 - 0.6 * math.exp(-0.3 * l)
        proj = h @ w_in[l]
        q, k, v, u, s, g_a, g_b = jnp.split(proj, SPLIT_POINTS, axis=-1)
        q = partial_rope(q.reshape(B, S, ATTN_HEADS, 2, ATTN_HEAD_DIM), pos)
        k = partial_rope(k.reshape(B, S, ATTN_HEADS, 2, ATTN_HEAD_DIM), pos)
        v = v.reshape(B, S, ATTN_HEADS, ATTN_V_DIM)
        lam = (jnp.exp(jnp.sum(lambda_q1[l].astype(jnp.float32) * lambda_k1[l].astype(jnp.float32)))
               - jnp.exp(jnp.sum(lambda_q2[l].astype(jnp.float32) * lambda_k2[l].astype(jnp.float32)))
               + lambda_init)
        y_a = diff_attention(q, k, v, lam, subln_w[l], lambda_init)
        y_b = chunked_sgu(jax.nn.gelu(u), jax.nn.gelu(s), sgu_ln_g[l], sgu_ln_b[l],
                          w_spatial[l], b_spatial[l])
        merged = (jax.nn.sigmoid(g_a) * (y_a @ w_proj_attn[l])
                  + jax.nn.sigmoid(g_b) * (y_b @ w_proj_sgu[l]))
        mix = merged @ w_out[l]
        h = layer_norm(ALPHA * h + mix, ln1_g[l], ln1_b[l])
        z = jax.nn.relu(h @ w_mlp_in[l] + b_mlp_in[l])
        ff = (z * z) @ w_mlp_out[l] + b_mlp_out[l]
        h = layer_norm(ALPHA * h + ff, ln2_g[l], ln2_b[l])
    return h
```

```python
import math
from contextlib import ExitStack

import numpy as np
import concourse.bass as bass
import concourse.mybir as mybir
from concourse.bass_utils import run_bass_kernel_spmd

F32 = mybir.dt.float32
BF16 = mybir.dt.bfloat16
AF = mybir.ActivationFunctionType
ALU = mybir.AluOpType

NCORES = 8
D = 4096
S = 8192
TOK = S // NCORES
NH = 8
DH = 128
DV = 256
SGU_W = 2048
DFF = 16384
ALPHA = 2.0 ** 0.25
LN_EPS = 1e-5
LAMBDA_INIT = 0.2
ROPE_THETA = 500000.0
NEG = -30000.0
SBUF_BYTES = 207 * 1024
import os
NT_DBG = int(os.environ.get('NT_DBG', '16'))
DO_ROPE = int(os.environ.get('DO_ROPE', '1'))
DO_V = int(os.environ.get('DO_V', '1'))


class Buf:
    def __init__(self, name, excl=False):
        self.name = name
        self.excl = excl
        self.lastw = None
        self.readers = []
        self.dsem = None
        self.dcnt = 0


class Sched:
    ENGS = ("pe", "act", "dve", "pool", "sp")

    def __init__(self, nc, stack):
        self.nc = nc
        self.stack = stack
        self.ops = {e: [] for e in self.ENGS}
        self.cnt = {e: 0 for e in self.ENGS}
        self.sem = {e: stack.enter_context(nc.semaphore("s_" + e)) for e in self.ENGS}
        self.waited = {e: {} for e in self.ENGS}
        self.nsem = 0

    def new_dsem(self, name):
        self.nsem += 1
        return self.stack.enter_context(self.nc.semaphore("d%d_%s" % (self.nsem, name)))

    def _wait(self, eng, tok):
        _, sem, val = tok
        key = id(sem)
        if self.waited[eng].get(key, 0) >= val:
            return
        self.waited[eng][key] = val
        self.ops[eng].append(lambda e, sem=sem, val=val: e.wait_ge(sem, val))

    def _deps(self, eng, reads, writes, same_eng_raw):
        for b in reads:
            w = b.lastw
            if w is not None and (w[0] != eng or same_eng_raw):
                self._wait(eng, w)
            if b.excl:
                for r in b.readers:
                    if r[0] != eng:
                        self._wait(eng, r)
        for b in writes:
            for r in b.readers:
                if r[0] != eng:
                    self._wait(eng, r)
            w = b.lastw
            if w is not None and w[0] != eng:
                self._wait(eng, w)

    def op(self, eng, fn, reads=(), writes=(), sig=True):
        self._deps(eng, reads, writes, same_eng_raw=(eng != "pe"))
        sem = self.sem[eng]
        if sig:
            self.cnt[eng] += 1
            tok = (eng, sem, self.cnt[eng])
            self.ops[eng].append(lambda e, fn=fn, sem=sem: fn(e).then_inc(sem, 1))
        else:
            tok = (eng, sem, self.cnt[eng] + 1)
            self.ops[eng].append(lambda e, fn=fn: fn(e))
        for b in reads:
            b.readers.append(tok)
        for b in writes:
            b.lastw = tok
            b.readers = []

    def dma(self, q, out, in_, reads=(), writes=(), track=None):
        self._deps(q, reads, writes, same_eng_raw=True)
        tb = track if track is not None else (writes[0] if writes else reads[0])
        if tb.dsem is None:
            tb.dsem = self.new_dsem(tb.name)
        tb.dcnt += 16
        tok = ("dma", tb.dsem, tb.dcnt)
        sem = tb.dsem
        self.ops[q].append(lambda e, out=out, in_=in_, sem=sem: e.dma_start(out=out, in_=in_).then_inc(sem, 16))
        for b in reads:
            b.readers.append(tok)
        for b in writes:
            b.lastw = tok
            b.readers = []
        return tok

    def barrier(self, bufs):
        toks = [(e, self.sem[e], self.cnt[e]) for e in ("pe", "act", "dve", "pool") if self.cnt[e] > 0]
        for b in bufs:
            if b.lastw is not None and b.lastw[0] == "dma":
                toks.append(b.lastw)
            for r in b.readers:
                if r[0] == "dma":
                    toks.append(r)
        for e in self.ENGS:
            for t in toks:
                if t[0] != e:
                    self._wait(e, t)

    def wait_tok(self, eng, tok):
        self._wait(eng, tok)


def build_program(stage=99):
    nc = bass.Bass("TRN2", target_bir_lowering=False)

    def din(name, shape, dt=F32):
        return nc.dram_tensor(name, list(shape), dt, kind="ExternalInput").ap()

    xT = din("xT", [16, 128, 32 * 512])
    x_own = din("x_own", [TOK, D]) if stage >= 3 else None
    w_in = din("w_in", [72, 128, 32 * 256])
    ropeC = din("ropeC", [32, S])
    ropeS = din("ropeS", [32, S])
    maskbias = din("maskbias", [128, 128])
    masks4 = din("masks4", [128, 4 * 512])
    ident_in = din("ident", [128, 128])
    pm_in = din("pm", [128, 128])
    tril_in = din("trilT", [128, 128])
    lam_in = din("lam4", [4, 128])
    subln_in = din("subln_w", [1, DV])
    sgu_g_in = din("sgu_ln_g", [1, SGU_W])
    sgu_b_in = din("sgu_ln_b", [1, SGU_W])
    wsT_in = din("wsT", [128, 8 * 128])
    bS_in = din("b_spatial", [1, 8 * 128])
    w_pa = din("w_proj_attn", [16, 128, 16 * 256]) if stage >= 3 else None
    w_ps = din("w_proj_sgu", [16, 128, 16 * 256]) if stage >= 3 else None
    w_out = din("w_out", [16, 128, 32 * 256]) if stage >= 3 else None
    ln1_g = din("ln1_g", [1, D])
    ln1_b = din("ln1_b", [1, D])
    w1 = din("w_mlp_in", [64, 128, 32 * 256]) if stage >= 3 else None
    b1T_in = din("b1T", [128, 128])
    w2 = din("w_mlp_out", [DFF, D]) if stage >= 3 else None
    b2_in = din("b_mlp_out", [1, D])
    ln2_g = din("ln2_g", [1, D])
    ln2_b = din("ln2_b", [1, D])
    out = nc.dram_tensor("out", [TOK, D], F32, kind="ExternalOutput").ap()
    dbg = nc.dram_tensor("dbg", [TOK, 2048], F32, kind="ExternalOutput").ap() if stage < 99 else None
    yaT_d = nc.dram_tensor("yaT_d", [2048, TOK], BF16)

    with ExitStack() as stack:
        SB = stack.enter_context(nc.sbuf_tensor("sb", [128, SBUF_BYTES // 2], BF16))
        banks = [stack.enter_context(nc.psum_tensor("ps%d" % i, [128, 512], F32)) for i in range(8)]
        pbuf = [Buf("ps%d" % i, excl=True) for i in range(8)]
        sc = Sched(nc, stack)

        def carve(off, free_shape, dt):
            esz = 2 if dt == BF16 else 4
            n = 1
            for s_ in free_shape:
                n *= s_
            nbytes = n * esz
            assert off % 4 == 0 and off + nbytes <= SBUF_BYTES, (off, nbytes)
            ap = SB[:, off // 2:(off + nbytes) // 2]
            if dt != BF16:
                ap = ap.bitcast(dt)
            if len(free_shape) == 2:
                ap = ap.rearrange("p (a b) -> p a b", b=free_shape[1])
            elif len(free_shape) == 3:
                ap = ap.rearrange("p (a b c) -> p a b c", b=free_shape[1], c=free_shape[2])
            return ap

        CB = SBUF_BYTES - 8448
        o = CB
        ident = carve(o, [128], BF16); o += 256
        pm = carve(o, [128], BF16); o += 256
        mbias = carve(o, [128], F32); o += 512
        m4 = carve(o, [4, 512], BF16); o += 4096
        lam4 = carve(o, [4, 128], F32); o += 2048
        sublnw = carve(o, [DV], F32); o += 1024
        lamt = carve(o, [8], F32); o += 32
        epsc = carve(o, [1], F32); o += 4
        assert o <= SBUF_BYTES
        consts = Buf("consts")
        sc.dma("pool", ident, ident_in, writes=[consts])
        sc.dma("pool", pm, pm_in, writes=[consts])
        sc.dma("sp", mbias, maskbias, writes=[consts])
        sc.dma("pool", m4, masks4.rearrange("p (a b) -> p a b", b=512), writes=[consts])
        sc.dma("sp", lam4, lam_in.rearrange("a b -> (a b)").partition_broadcast(128).rearrange("p (a b) -> p a b", b=128),
               writes=[consts])
        sc.dma("sp", sublnw, subln_in.rearrange("a b -> (a b)").partition_broadcast(128), writes=[consts])

        sc.op("dve", lambda e: e.memset(epsc, LN_EPS), writes=[consts])
        lamb = Buf("lamb")
        junk128 = carve(CB - 512, [128], F32)
        jb = Buf("junk128")
        sc.op("dve", lambda e: e.tensor_tensor(out=junk128, in0=lam4[:, 0, :], in1=lam4[:, 1, :], op=ALU.mult),
              reads=[consts], writes=[jb])
        sc.op("dve", lambda e: e.tensor_reduce(out=lamt[:, 0:1], in_=junk128, axis=mybir.AxisListType.X, op=ALU.add),
              reads=[jb], writes=[lamb])
        sc.op("dve", lambda e: e.tensor_tensor(out=junk128, in0=lam4[:, 2, :], in1=lam4[:, 3, :], op=ALU.mult),
              reads=[consts, lamb], writes=[jb])
        sc.op("dve", lambda e: e.tensor_reduce(out=lamt[:, 1:2], in_=junk128, axis=mybir.AxisListType.X, op=ALU.add),
              reads=[jb], writes=[lamb])
        sc.op("act", lambda e: e.activation(out=lamt[:, 2:4], in_=lamt[:, 0:2], func=AF.Exp), reads=[lamb], writes=[lamb])
        sc.op("dve", lambda e: e.tensor_tensor(out=lamt[:, 4:5], in0=lamt[:, 2:3], in1=lamt[:, 3:4], op=ALU.subtract),
              reads=[lamb], writes=[lamb])
        sc.op("dve", lambda e: e.tensor_scalar(out=lamt[:, 5:6], in0=lamt[:, 4:5], scalar1=-1.0, scalar2=-LAMBDA_INIT,
                                               op0=ALU.mult, op1=ALU.add), reads=[lamb], writes=[lamb])

        def finish_dbg(src_ap, nrow, ncol, srcbuf):
            t_ = sc.dma("sp", dbg[0:nrow, 0:ncol], src_ap, reads=[srcbuf], writes=[], track=srcbuf)
            sc.wait_tok("sp", t_)
            _emit(nc, sc)
            return nc

        if stage == -2:
            return finish_dbg(lamt, 128, 8, lamb)

        o = 0
        XT = [carve(o, [32, 512], BF16), carve(o + 32768, [32, 512], BF16)]; o += 65536
        Wh = carve(o, [32, 768], BF16); o += 49152
        KT = carve(o, [2, S], BF16); o += 32768
        V = carve(o, [64, 257], BF16); o += 64 * 257 * 2
        QT = carve(o, [2, TOK], BF16); o += 4096
        PT = [carve(o + i * 1024, [512], BF16) for i in range(3)]; o += 3072
        RC = carve(o, [512], F32); o += 2048
        RS = carve(o, [512], F32); o += 2048
        rt1 = carve(o, [512], F32); o += 2048
        rt2 = carve(o, [512], F32); o += 2048
        oacc = carve(o, [4, DV], F32); o += 4096
        ytile4 = carve(o, [4, DV], BF16); o += 2048
        ytT = carve(o, [2, 128], BF16); o += 512
        stt = carve(o, [16], F32); o += 64
        st3 = carve(o, [16], F32); o += 64
        bst3 = Buf("st3")
        assert o <= CB - 512, o
        bXT = [Buf("XT0"), Buf("XT1")]
        bWh, bKT, bV, bQT = Buf("Wh"), Buf("KT"), Buf("V"), Buf("QT")
        bPT = [Buf("PT%d" % i) for i in range(3)]
        bRope, br32, brt1, brt2 = Buf("rope"), Buf("r32b"), Buf("rt1"), Buf("rt2")
        boacc, bytT, bstt = Buf("oacc"), Buf("ytT"), Buf("stt")
        bytile4 = [Buf("ytile%d" % i) for i in range(4)]
        deferred = []
        deferred_early = []

        def flush_early():
            for f_ in deferred_early:
                f_()
            del deferred_early[:]

        def flush_deferred():
            flush_early()
            for f_ in deferred:
                f_()
            del deferred[:]
        byaT = Buf("yaT_d")
        scale = DH ** -0.5

        sc.op("dve", lambda e: e.memset(V[:, :, 256:257], 1.0), writes=[bV])

        xt_i = [0]

        def rope_evac(ps_i, dst, dstbuf, t):
            ps = banks[ps_i]
            sc.op("act", lambda e: e.activation(out=dst, in_=ps[:, :], func=AF.Copy), reads=[pbuf[ps_i]], writes=[dstbuf])
            sc.op("pe", lambda e: e.matmul(out=banks[2][:, :], lhsT=pm, rhs=dst, start=True, stop=True),
                  reads=[dstbuf, consts], writes=[pbuf[2]])
            sc.op("dve", lambda e: e.tensor_tensor(out=rt1[0:32, :], in0=ps[0:32, :], in1=RC[0:32, :], op=ALU.mult),
                  reads=[pbuf[ps_i], bRope], writes=[brt1])
            sc.op("dve", lambda e: e.tensor_tensor(out=rt2[0:32, :], in0=banks[2][0:32, :], in1=RS[0:32, :], op=ALU.mult),
                  reads=[pbuf[2], bRope], writes=[brt2])
            sc.op("dve", lambda e: e.tensor_tensor(out=dst[0:32, :], in0=rt1[0:32, :], in1=rt2[0:32, :], op=ALU.add),
                  reads=[brt1, brt2], writes=[dstbuf])

        nheads = NH if stage >= 2 else 1
        for h in range(nheads):
            for part, c0 in enumerate((h * 256, 2048 + h * 256, 4096 + h * 256)):
                sc.dma("pool", Wh[:, :, part * 256:(part + 1) * 256],
                       w_in[c0 // 256].rearrange("p (kc c) -> p kc c", c=256), writes=[bWh])
            for t in range(16):
                slot = xt_i[0] % 2
                xt_i[0] += 1
                sc.dma("pool", XT[slot], xT[t].rearrange("p (kc t) -> p kc t", t=512), writes=[bXT[slot]])
                sc.dma("sp", RC[0:32, :], ropeC[:, t * 512:(t + 1) * 512], writes=[bRope])
                sc.dma("sp", RS[0:32, :], ropeS[:, t * 512:(t + 1) * 512], writes=[bRope])
                if stage == -3 and t >= NT_DBG:
                    break
                groups = [("k", 0), ("k", 1)] + ([("q", 0), ("q", 1)] if t < 2 else [])
                for gi, (kind, m) in enumerate(groups):
                    pi = gi % 2
                    c0 = (256 if kind == "k" else 0) + m * 128
                    for kc in range(32):
                        sc.op("pe", lambda e, pi=pi, kc=kc, c0=c0, slot=slot: e.matmul(
                            out=banks[pi][:, :], lhsT=Wh[:, kc, c0:c0 + 128], rhs=XT[slot][:, kc, :],
                            start=(kc == 0), stop=(kc == 31)),
                            reads=[bWh, bXT[slot]], writes=[pbuf[pi]], sig=(kc == 31))
                    if stage == -3 and not DO_ROPE:
                        dst_ = (KT if kind == "k" else QT)[:, m, t * 512:(t + 1) * 512]
                        sc.op("act", lambda e, pi=pi, dst_=dst_: e.activation(out=dst_, in_=banks[pi][:, :], func=AF.Copy),
                              reads=[pbuf[pi]], writes=[bKT if kind == "k" else bQT])
                    elif kind == "k":
                        rope_evac(pi, KT[:, m, t * 512:(t + 1) * 512], bKT, t)
                    else:
                        rope_evac(pi, QT[:, m, t * 512:(t + 1) * 512], bQT, t)
                if t == 0:
                    flush_early()
                if t == 1:
                    flush_deferred()
                for sub in range(4):
                    if stage == -3 and not DO_V:
                        break
                    pi = sub % 2
                    for kc in range(32):
                        sc.op("pe", lambda e, pi=pi, kc=kc, sub=sub, slot=slot: e.matmul(
                            out=banks[pi][:, 0:256], lhsT=XT[slot][:, kc, sub * 128:(sub + 1) * 128],
                            rhs=Wh[:, kc, 512:768], start=(kc == 0), stop=(kc == 31)),
                            reads=[bWh, bXT[slot]], writes=[pbuf[pi]], sig=(kc == 31))
                    j = t * 4 + sub
                    sc.op("act", lambda e, pi=pi, j=j: e.activation(out=V[:, j, 0:256], in_=banks[pi][:, 0:256], func=AF.Copy),
                          reads=[pbuf[pi]], writes=[bV])

            if stage in (-1, -3):
                kt32 = carve(0, [TOK], F32)
                bk = Buf("kt32")
                sc.op("dve", lambda e: e.tensor_copy(out=kt32, in_=KT[:, 0, 0:TOK]), reads=[bKT, bXT[0], bXT[1]], writes=[bk])
                return finish_dbg(kt32, 128, TOK, bk)
            for qh in range(2):
                for m in range(2):
                    if qh == 0:
                        jlist = [(j, j * 128) for j in range(4)] + [(j, None) for j in range(8, 36)]
                    else:
                        jlist = ([(j, None) for j in range(4)] + [(j, (j - 4) * 128) for j in range(4, 8)]
                                 + [(j, None) for j in range(8, 64)])
                    n_t = len(jlist)

                    def emit_s(idx, m=m, qh=qh, jlist=jlist):
                        j, off = jlist[idx]
                        sb_i = idx % 4
                        p_i = idx % 3
                        sc.op("pe", lambda e: e.matmul(
                            out=banks[sb_i][:, :], lhsT=KT[:, m, j * 128:(j + 1) * 128],
                            rhs=QT[:, m, qh * 512:(qh + 1) * 512], start=True, stop=True),
                            reads=[bKT, bQT], writes=[pbuf[sb_i]])
                        sc.op("act", lambda e: e.activation(
                            out=PT[p_i], in_=banks[sb_i][:, :], func=AF.Exp, bias=mbias[:, qh * 64 + j:qh * 64 + j + 1], scale=scale),
                            reads=[pbuf[sb_i], consts], writes=[bPT[p_i]])
                        if off is not None:
                            mi = off // 128
                            sc.op("dve", lambda e: e.tensor_tensor(
                                out=PT[p_i], in0=PT[p_i], in1=m4[:, mi, :], op=ALU.mult),
                                reads=[bPT[p_i], consts], writes=[bPT[p_i]])

                    def emit_pv(idx, jlist=jlist, n_t=n_t):
                        j, off = jlist[idx]
                        p_i = idx % 3
                        for qs in range(4):
                            if off is not None and qs * 128 + 127 < off:
                                if idx != 0 and idx != n_t - 1:
                                    continue
                            sc.op("pe", lambda e, qs=qs: e.matmul(
                                out=banks[4 + qs][:, 0:257], lhsT=PT[p_i][:, qs * 128:(qs + 1) * 128],
                                rhs=V[:, j, :], start=(idx == 0), stop=(idx == n_t - 1)),
                                reads=[bPT[p_i], bV], writes=[pbuf[4 + qs]], sig=True)

                    LA = 2
                    for idx in range(n_t + LA):
                        if idx < n_t:
                            emit_s(idx)
                        if idx - LA >= 0:
                            emit_pv(idx - LA)
                        if idx == 16:
                            flush_early()
                        if idx == 40:
                            flush_deferred()
                    flush_deferred()
                    for qs in range(4):
                        ob = banks[4 + qs]
                        sc.op("dve", lambda e, ob=ob, m=m: e.reciprocal(out=stt[:, m:m + 1], in_=ob[:, 256:257]),
                              reads=[pbuf[4 + qs]], writes=[bstt])
                        if m == 0:
                            sc.op("dve", lambda e, ob=ob, qs=qs: e.tensor_scalar(
                                out=oacc[:, qs, :], in0=ob[:, 0:256], scalar1=stt[:, 0:1], scalar2=None, op0=ALU.mult),
                                reads=[pbuf[4 + qs], bstt], writes=[boacc])
                        else:
                            sc.op("dve", lambda e: e.tensor_tensor(out=stt[:, 2:3], in0=stt[:, 1:2], in1=lamt[:, 5:6], op=ALU.mult),
                                  reads=[bstt, lamb], writes=[bstt])
                            sc.op("dve", lambda e, ob=ob, qs=qs: e.scalar_tensor_tensor(
                                out=oacc[:, qs, :], in0=ob[:, 0:256], scalar=stt[:, 2:3], in1=oacc[:, qs, :],
                                op0=ALU.mult, op1=ALU.add), reads=[pbuf[4 + qs], bstt, boacc], writes=[boacc])
                    if m == 1:
                        for qs in range(4):
                            sc.op("dve", lambda e, qs=qs: e.bn_stats(out=stt[:, 4:10], in_=oacc[:, qs, :]),
                                  reads=[boacc], writes=[bstt])
                            sc.op("dve", lambda e: e.bn_aggr(out=stt[:, 10:12], in_=stt[:, 4:10]), reads=[bstt], writes=[bstt])
                            sc.op("dve", lambda e: e.tensor_tensor(out=stt[:, 12:13], in0=stt[:, 10:11], in1=stt[:, 10:11], op=ALU.mult),
                                  reads=[bstt], writes=[bstt])
                            sc.op("dve", lambda e, qs=qs: e.tensor_tensor(out=st3[:, qs:qs + 1], in0=stt[:, 12:13], in1=stt[:, 11:12], op=ALU.add),
                                  reads=[bstt], writes=[bst3])

                        def norm_tail():
                            sc.op("act", lambda e: e.activation(out=st3[:, 4:8], in_=st3[:, 0:4], func=AF.Sqrt, bias=epsc[:, 0:1], scale=1.0),
                                  reads=[bst3, consts], writes=[bst3])
                            sc.op("dve", lambda e: e.reciprocal(out=st3[:, 8:12], in_=st3[:, 4:8]), reads=[bst3], writes=[bst3])
                            for qs_ in range(4):
                                sc.op("dve", lambda e, qs_=qs_: e.scalar_tensor_tensor(
                                    out=oacc[:, qs_, :], in0=oacc[:, qs_, :], scalar=st3[:, 8 + qs_:9 + qs_], in1=sublnw,
                                    op0=ALU.mult, op1=ALU.mult), reads=[boacc, bst3, consts], writes=[boacc])
                                sc.op("dve", lambda e, qs_=qs_: e.tensor_scalar(
                                    out=ytile4[:, qs_, :], in0=oacc[:, qs_, :], scalar1=1.0 - LAMBDA_INIT, scalar2=None, op0=ALU.mult),
                                    reads=[boacc], writes=[bytile4[qs_]])
                        deferred_early.append(norm_tail)
                        for qs in range(4):

                            def tail(qs=qs, qh=qh, h=h):
                                tb = banks[2].bitcast(BF16)
                                for hh in range(2):
                                    sc.op("pe", lambda e, hh=hh: e.transpose(
                                        out=tb[:, hh * 128:(hh + 1) * 128], in_=ytile4[:, qs, hh * 128:(hh + 1) * 128], identity=ident),
                                        reads=[bytile4[qs], consts], writes=[pbuf[2]])
                                sc.op("act", lambda e: e.activation(out=ytT, in_=tb[:, 0:256].rearrange("p (a b) -> p a b", b=128),
                                                                    func=AF.Copy), reads=[pbuf[2]], writes=[bytT])
                                tok0 = qh * 512 + qs * 128
                                sc.dma("sp", yaT_d[h * 256:(h + 1) * 256, tok0:tok0 + 128].rearrange("(a p) t -> p a t", p=128),
                                       ytT, reads=[bytT], writes=[byaT], track=byaT)
                            deferred.append(tail)
        flush_deferred()

        if stage < 3:
            dbt = carve(0, [TOK], F32)
            dbb = carve(8192, [TOK], BF16)
            bdb = Buf("dbg")
            sc.barrier([byaT, bXT[0], bXT[1], bWh])
            for c in range(2 * nheads):
                sc.dma("sp", dbb, yaT_d[c * 128:(c + 1) * 128, :], reads=[byaT], writes=[bdb])
                sc.op("dve", lambda e: e.tensor_copy(out=dbt, in_=dbb), reads=[bdb], writes=[bdb])
                t_ = sc.dma("sp", dbg[:, c * 128:(c + 1) * 128].rearrange("(a p) f -> p a f", p=128) if False else
                            dbg[c * 128:(c + 1) * 128, 0:TOK], dbt, reads=[bdb], writes=[], track=bdb)
            sc.wait_tok("sp", t_)
            _emit(nc, sc)
            return nc

        sc.barrier([byaT, bXT[0], bXT[1], bWh, bRope, bytT, bKT, bV, bQT])
        bR = [Buf("R%d" % i) for i in range(4)]
        bM = [Buf("M0"), Buf("M1")]
        bX = [Buf("X0"), Buf("X1")]
        o = 0
        R = carve(0, [4, D], F32); o += 65536
        ug = carve(0, [16, 512], BF16)
        sln = carve(16384, [4, SGU_W], BF16)
        ybT = carve(32768, [16, 512], BF16)
        yaT = carve(49152, [16, 512], BF16)
        sg4 = carve(32768, [4, SGU_W], F32)
        MO = o; o += 32768
        sgug = carve(MO + 16384, [SGU_W], F32)
        sgub = carve(MO + 24576, [SGU_W], F32)
        mT = carve(MO, [32, 512], BF16)
        XO = o; o += 32768
        XTo = carve(XO, [32, 512], BF16)
        PB = [carve(XO, [D], F32), carve(XO + 16384, [D], F32)]
        PB2 = [carve(MO, [D], F32), carve(MO + 16384, [D], F32)]
        ring_offs = [o, o + 16384, o + 32768, XO, XO + 16384]
        ring = [carve(off_, [32, 256], BF16) for off_ in ring_offs]
        ringW2 = [carve(off_, [2, D], BF16) for off_ in ring_offs]
        o += 49152
        bring = [Buf("ring0"), Buf("ring1"), Buf("ring2"), bX[0], bX[1]]
        zT = [carve(o, [4, 512], BF16), carve(o + 4096, [4, 512], BF16)]; o += 8192
        bzT = [Buf("zT0"), Buf("zT1")]
        bS = carve(o, [8, 128], F32); o += 4096
        wsT = carve(o, [8, 128], BF16); o += 2048
        tril = carve(o, [128], BF16); o += 256
        b1T = carve(o, [128], F32); o += 512
        identf = carve(o, [128], F32); o += 512
        gt_off = o
        gt = [carve(o + i * 2048, [512], F32) for i in range(3)]; o += 6144
        bgt = Buf("gt")
        st = carve(o, [64], F32); o += 256
        bst = Buf("st")
        sgm = [carve(o - 256 - 6144 - 512 - 512 - 256 - 2048 - 4096 - 8192, [2, 512], BF16),
               carve(o - 256 - 6144 - 512 - 512 - 256 - 2048 - 4096 - 8192 + 2048, [2, 512], BF16)]
        bsgm = [bzT[0], bzT[0]]
        p1 = carve(gt_off + 2048, [2, 512], F32)
        bp1 = bgt
        p2 = gt[0]
        bp2 = bgt
        assert o <= CB - 512, o
        cB = Buf("constsB")
        sc.dma("pool", wsT, wsT_in.rearrange("p (g t) -> p g t", t=128), writes=[cB])
        sc.dma("pool", tril, tril_in, writes=[cB])
        sc.dma("sp", bS, bS_in.rearrange("a b -> (a b)").partition_broadcast(128).rearrange("p (g t) -> p g t", t=128), writes=[cB])
        sc.dma("sp", b1T, b1T_in, writes=[cB])
        sc.dma("sp", identf, ident_in, writes=[cB])
        for g in range(8):
            sc.op("dve", lambda e, g=g: e.tensor_tensor(out=wsT[:, g, :], in0=wsT[:, g, :], in1=tril, op=ALU.mult),
                  reads=[cB], writes=[cB])

        bank_i = [0]

        def nb():
            bank_i[0] = (bank_i[0] + 1) % 8
            return bank_i[0]

        blocks = []
        for th in range(2):
            for b in range(8):
                blocks.append((w_in[32 + b], 32, 0))
            for b in range(8):
                blocks.append((w_in[24 + b], 32, 0))
            for cb in range(16):
                blocks.append((w_in[40 + cb], 32, 0))
                blocks.append((w_pa[cb], 16, 0))
                blocks.append((w_in[56 + cb], 32, 0))
                blocks.append((w_ps[cb], 16, 0))
            for cb in range(16):
                blocks.append((w_out[cb], 32, 0))
            def _a(pr):
                for ffb in (2 * pr, 2 * pr + 1):
                    blocks.append((w1[ffb], 32, 0, 1))

            def _b(pr):
                for ffb in (2 * pr, 2 * pr + 1):
                    blocks.append((w2[ffb * 256:(ffb + 1) * 256, :], 2, 1, 1))
            _a(0)
            for pr in range(32):
                if pr + 1 < 32:
                    _a(pr + 1)
                _b(pr)
        last_use = [-10 - i for i in range(5)]
        slot_of = []
        mlp_run = 0
        for k_, blk_ in enumerate(blocks):
            mlp_run = mlp_run + 1 if len(blk_) > 3 else 0
            allowed = range(5) if mlp_run > 2 else range(3)
            s_ = min(allowed, key=lambda a: last_use[a])
            assert k_ - last_use[s_] >= 3
            last_use[s_] = k_
            slot_of.append(s_)
        issued = [0]
        used = [0]

        def next_block():
            i = used[0]
            used[0] += 1
            while issued[0] < min(i + 3, len(blocks)):
                k_ = issued[0]
                src, nk, kind = blocks[k_][:3]
                sl = slot_of[k_]
                if kind == 0:
                    sc.dma("pool", ring[sl][:, 0:nk, :], src.rearrange("p (kc c) -> p kc c", c=256), writes=[bring[sl]])
                else:
                    sc.dma("pool", ringW2[sl], src.rearrange("(fc p) d -> p fc d", p=128), writes=[bring[sl]])
                issued[0] += 1
            return slot_of[i]

        def fm_group(sl, nk, c, rhsT, rbufs, bank):
            for kc in range(nk):
                sc.op("pe", lambda e, kc=kc: e.matmul(out=banks[bank][:, :], lhsT=ring[sl][:, kc, c * 128:(c + 1) * 128],
                                                      rhs=rhsT[:, kc, :], start=(kc == 0), stop=(kc == nk - 1)),
                      reads=[bring[sl]] + rbufs, writes=[pbuf[bank]], sig=(kc == nk - 1))

        def tm_group(sl, nk, lsrc, lbufs, tt, bank):
            for kc in range(nk):
                sc.op("pe", lambda e, kc=kc: e.matmul(out=banks[bank][:, 0:256], lhsT=lsrc[:, kc, tt * 128:(tt + 1) * 128],
                                                      rhs=ring[sl][:, kc, :], start=(kc == 0), stop=(kc == nk - 1)),
                      reads=[bring[sl]] + lbufs, writes=[pbuf[bank]], sig=(kc == nk - 1))

        def gelu(dst, dbufs, bank, n):
            ps = banks[bank][:, 0:n]
            sc.op("act", lambda e: e.activation(out=gt[0][:, 0:n], in_=ps, func=AF.Copy), reads=[pbuf[bank]], writes=[bgt])
            sc.op("dve", lambda e: e.tensor_tensor(out=gt[1][:, 0:n], in0=gt[0][:, 0:n], in1=gt[0][:, 0:n], op=ALU.mult),
                  reads=[bgt], writes=[bgt])
            sc.op("dve", lambda e: e.tensor_scalar(out=gt[1][:, 0:n], in0=gt[1][:, 0:n], scalar1=0.044715, scalar2=1.0,
                                                   op0=ALU.mult, op1=ALU.add), reads=[bgt], writes=[bgt])
            sc.op("dve", lambda e: e.tensor_tensor(out=gt[1][:, 0:n], in0=gt[1][:, 0:n], in1=gt[0][:, 0:n], op=ALU.mult),
                  reads=[bgt], writes=[bgt])
            sc.op("act", lambda e: e.activation(out=gt[2][:, 0:n], in_=gt[1][:, 0:n], func=AF.Sigmoid, scale=1.5957691216057308),
                  reads=[bgt], writes=[bgt])
            sc.op("dve", lambda e: e.tensor_tensor(out=dst, in0=gt[0][:, 0:n], in1=gt[2][:, 0:n], op=ALU.mult),
                  reads=[bgt], writes=dbufs)

        def layer_norm(xap, xbufs, nfeat, g_ap, b_ap, gbufs, out_ap, obufs):
            nch = nfeat // 512
            for ci in range(nch):
                sc.op("dve", lambda e, ci=ci: e.bn_stats(out=st[:, ci * 6:(ci + 1) * 6], in_=xap[:, ci * 512:(ci + 1) * 512]),
                      reads=xbufs, writes=[bst])
            sc.op("dve", lambda e: e.bn_aggr(out=st[:, 48:50], in_=st[:, 0:nch * 6]), reads=[bst], writes=[bst])
            sc.op("act", lambda e: e.activation(out=st[:, 50:51], in_=st[:, 49:50], func=AF.Sqrt, bias=epsc[:, 0:1], scale=1.0),
                  reads=[bst, consts], writes=[bst])
            sc.op("dve", lambda e: e.reciprocal(out=st[:, 51:52], in_=st[:, 50:51]), reads=[bst], writes=[bst])
            sc.op("dve", lambda e: e.tensor_scalar(out=xap, in0=xap, scalar1=st[:, 48:49], scalar2=st[:, 51:52],
                                                   op0=ALU.subtract, op1=ALU.mult), reads=xbufs + [bst], writes=xbufs)
            sc.op("dve", lambda e: e.tensor_tensor(out=xap, in0=xap, in1=g_ap, op=ALU.mult), reads=xbufs + gbufs, writes=xbufs)
            sc.op("dve", lambda e: e.tensor_tensor(out=out_ap, in0=xap, in1=b_ap, op=ALU.add), reads=xbufs + gbufs, writes=obufs)

        bout = [Buf("out%d" % i) for i in range(4)]
        last_out = [None] * 4
        for th in range(2):
            t0 = th * 512
            sc.dma("pool", XTo, xT[th].rearrange("p (kc t) -> p kc t", t=512), writes=[bX[0], bX[1]])
            sc.dma("sp", sgug, sgu_g_in.rearrange("a b -> (a b)").partition_broadcast(128), writes=[bM[1]])
            sc.dma("sp", sgub, sgu_b_in.rearrange("a b -> (a b)").partition_broadcast(128), writes=[bM[1]])
            for b in range(8):
                sl = next_block()
                for tt in range(4):
                    bk = nb()
                    tm_group(sl, 32, XTo, [bX[0], bX[1]], tt, bk)
                    gelu(sg4[:, tt, b * 256:(b + 1) * 256], [bR[2 + tt // 2]], bk, 256)
            for tt in range(4):
                layer_norm(sg4[:, tt, :], [bR[2 + tt // 2]], SGU_W, sgug, sgub, [bM[1]], sln[:, tt, :], [bR[1]])
            for b in range(8):
                sl = next_block()
                for c in range(2):
                    bk = nb()
                    fm_group(sl, 32, c, XTo, [bX[0], bX[1]], bk)
                    gelu(ug[:, b * 2 + c, :], [bR[0]], bk, 512)
            for g in range(8):
                for dh in range(2):
                    bk = nb()
                    ch = g * 2 + dh
                    for tt in range(4):
                        sc.op("pe", lambda e, tt=tt, g=g, ch=ch, bk=bk: e.matmul(
                            out=banks[bk][:, tt * 128:(tt + 1) * 128], lhsT=sln[:, tt, ch * 128:(ch + 1) * 128],
                            rhs=wsT[:, g, :], start=True, stop=True),
                            reads=[bR[1], cB], writes=[pbuf[bk]], sig=(tt == 3))
                    for tt in range(4):
                        sc.op("dve", lambda e, tt=tt, g=g, bk=bk: e.tensor_tensor(
                            out=p1[:, 0, tt * 128:(tt + 1) * 128], in0=banks[bk][:, tt * 128:(tt + 1) * 128], in1=bS[:, g, :],
                            op=ALU.add), reads=[pbuf[bk], cB], writes=[bp1])
                    sc.op("dve", lambda e, ch=ch: e.tensor_tensor(out=ybT[:, ch, :], in0=p1[:, 0, :], in1=ug[:, ch, :], op=ALU.mult),
                          reads=[bp1, bR[0]], writes=[bR[2]])
            sc.dma("sp", yaT, yaT_d[:, t0:t0 + 512].rearrange("(kc p) t -> p kc t", p=128), reads=[byaT], writes=[bR[3]])
            for cb in range(16):
                sl = next_block()
                for c in range(2):
                    bk = nb()
                    fm_group(sl, 32, c, XTo, [bX[0], bX[1]], bk)
                    sc.op("act", lambda e, c=c, bk=bk: e.activation(out=sgm[0][:, c, :], in_=banks[bk][:, :], func=AF.Sigmoid),
                          reads=[pbuf[bk]], writes=[bsgm[0]])
                sl = next_block()
                for c in range(2):
                    bk = nb()
                    fm_group(sl, 16, c, yaT, [bR[3]], bk)
                    sc.op("dve", lambda e, c=c, bk=bk: e.tensor_tensor(out=p1[:, c, :], in0=banks[bk][:, :], in1=sgm[0][:, c, :], op=ALU.mult),
                          reads=[pbuf[bk], bsgm[0]], writes=[bp1])
                sl = next_block()
                for c in range(2):
                    bk = nb()
                    fm_group(sl, 32, c, XTo, [bX[0], bX[1]], bk)
                    sc.op("act", lambda e, c=c, bk=bk: e.activation(out=sgm[1][:, c, :], in_=banks[bk][:, :], func=AF.Sigmoid),
                          reads=[pbuf[bk]], writes=[bsgm[1]])
                sl = next_block()
                for c in range(2):
                    bk = nb()
                    fm_group(sl, 16, c, ybT, [bR[2]], bk)
                    sc.op("dve", lambda e, c=c, bk=bk: e.tensor_tensor(out=p2, in0=banks[bk][:, :], in1=sgm[1][:, c, :], op=ALU.mult),
                          reads=[pbuf[bk], bsgm[1]], writes=[bp2])
                    sc.op("dve", lambda e, c=c, cb=cb: e.tensor_tensor(out=mT[:, cb * 2 + c, :], in0=p2, in1=p1[:, c, :], op=ALU.add),
                          reads=[bp2, bp1], writes=[bM[0], bM[1]])
            for tt in range(4):
                sc.dma("sp", R[:, tt, :], x_own[t0 + tt * 128:t0 + (tt + 1) * 128, :], writes=[bR[tt]])
            for cb in range(16):
                sl = next_block()
                for tt in range(4):
                    bk = nb()
                    tm_group(sl, 32, mT, [bM[0], bM[1]], tt, bk)
                    sc.op("dve", lambda e, tt=tt, cb=cb, bk=bk: e.scalar_tensor_tensor(
                        out=R[:, tt, cb * 256:(cb + 1) * 256], in0=R[:, tt, cb * 256:(cb + 1) * 256], scalar=ALPHA,
                        in1=banks[bk][:, 0:256], op0=ALU.mult, op1=ALU.add), reads=[pbuf[bk], bR[tt]], writes=[bR[tt]])
            sc.dma("sp", PB[0], ln1_g.rearrange("a b -> (a b)").partition_broadcast(128), writes=[bX[0]])
            sc.dma("sp", PB[1], ln1_b.rearrange("a b -> (a b)").partition_broadcast(128), writes=[bX[1]])
            for tt in range(4):
                layer_norm(R[:, tt, :], [bR[tt]], D, PB[0], PB[1], [bX[0], bX[1]], R[:, tt, :], [bR[tt]])
                for k4 in range(8):
                    bk = nb()
                    for i in range(4):
                        kc = k4 * 4 + i
                        sc.op("pe", lambda e, i=i, kc=kc, tt=tt, bk=bk: e.transpose(
                            out=banks[bk][:, i * 128:(i + 1) * 128], in_=R[:, tt, kc * 128:(kc + 1) * 128], identity=identf),
                            reads=[bR[tt], cB], writes=[pbuf[bk]], sig=(i == 3))
                    sc.op("act", lambda e, k4=k4, tt=tt, bk=bk: e.activation(
                        out=mT[:, k4 * 4:(k4 + 1) * 4, tt * 128:(tt + 1) * 128],
                        in_=banks[bk][:, :].rearrange("p (a b) -> p a b", b=128), func=AF.Copy),
                        reads=[pbuf[bk]], writes=[bM[0], bM[1]])
            sc.dma("sp", PB[0], b2_in.rearrange("a b -> (a b)").partition_broadcast(128), writes=[bX[0]])
            for tt in range(4):
                sc.op("dve", lambda e, tt=tt: e.scalar_tensor_tensor(out=R[:, tt, :], in0=R[:, tt, :], scalar=ALPHA, in1=PB[0],
                                                                    op0=ALU.mult, op1=ALU.add), reads=[bR[tt], bX[0]], writes=[bR[tt]])
            def z_part(pr):
                zi = pr % 2
                a_sl = [next_block(), next_block()]
                for q in range(4):
                    bk = nb()
                    fm_group(a_sl[q // 2], 32, q % 2, mT, [bM[0], bM[1]], bk)
                    fcol = pr * 4 + q
                    sc.op("act", lambda e, bk=bk, fcol=fcol: e.activation(out=gt[0], in_=banks[bk][:, :], func=AF.Relu,
                                                                          bias=b1T[:, fcol:fcol + 1], scale=1.0),
                          reads=[pbuf[bk], cB], writes=[bgt])
                    sc.op("dve", lambda e, zi=zi, q=q: e.tensor_tensor(out=zT[zi][:, q, :], in0=gt[0], in1=gt[0], op=ALU.mult),
                          reads=[bgt], writes=[bzT[zi]])

            def ff_part(pr):
                zi = pr % 2
                b_sl = [next_block(), next_block()]
                for tt in range(4):
                    for db in range(8):
                        bk = nb()
                        for fc in range(4):
                            sl2 = b_sl[fc // 2]
                            sc.op("pe", lambda e, fc=fc, tt=tt, db=db, bk=bk, zi=zi, sl2=sl2: e.matmul(
                                out=banks[bk][:, :], lhsT=zT[zi][:, fc, tt * 128:(tt + 1) * 128],
                                rhs=ringW2[sl2][:, fc % 2, db * 512:(db + 1) * 512], start=(fc == 0), stop=(fc == 3)),
                                reads=[bzT[zi], bring[sl2]], writes=[pbuf[bk]], sig=(fc == 3))
                        sc.op("dve", lambda e, tt=tt, db=db, bk=bk: e.tensor_tensor(
                            out=R[:, tt, db * 512:(db + 1) * 512], in0=R[:, tt, db * 512:(db + 1) * 512], in1=banks[bk][:, :],
                            op=ALU.add), reads=[pbuf[bk], bR[tt]], writes=[bR[tt]])

            z_part(0)
            for pr in range(32):
                if pr + 1 < 32:
                    z_part(pr + 1)
                ff_part(pr)
            sc.dma("sp", PB2[0], ln2_g.rearrange("a b -> (a b)").partition_broadcast(128), writes=[bM[0]])
            sc.dma("sp", PB2[1], ln2_b.rearrange("a b -> (a b)").partition_broadcast(128), writes=[bM[1]])
            for tt in range(4):
                layer_norm(R[:, tt, :], [bR[tt]], D, PB2[0], PB2[1], [bM[0], bM[1]], R[:, tt, :], [bR[tt]])
                r0 = t0 + tt * 128
                last_out[tt] = sc.dma("sp", out[r0:r0 + 128, :], R[:, tt, :], reads=[bR[tt]], writes=[], track=bout[tt])
        for tok_ in last_out:
            sc.wait_tok("sp", tok_)
        _emit(nc, sc)
        return nc


def _emit(nc, sc):
    with nc.Block() as block:
        @block.tensor
        def _(e):
            for f in sc.ops["pe"]:
                f(e)

        @block.scalar
        def _(e):
            for f in sc.ops["act"]:
                f(e)

        @block.vector
        def _(e):
            for f in sc.ops["dve"]:
                f(e)

        @block.gpsimd
        def _(e):
            for f in sc.ops["pool"]:
                f(e)

        @block.sync
        def _(e):
            for f in sc.ops["sp"]:
                f(e)


def _host_inputs(inp):
    x = np.asarray(inp["x"], np.float32).reshape(S, D)
    half = 8
    inv_freq = (ROPE_THETA ** (-np.arange(0, 32, 2, dtype=np.float32) / np.float32(32))).astype(np.float32)
    pos = np.arange(S, dtype=np.float32)
    ang = (pos[:, None] * inv_freq[None, :]).astype(np.float32)
    cosT = np.cos(ang).astype(np.float32).T
    sinT = np.sin(ang).astype(np.float32).T
    C_full = np.concatenate([cosT, cosT], 0)
    S_full = np.concatenate([-sinT, sinT], 0)
    pmat = np.zeros((128, 128), np.float32)
    for m_ in range(32):
        pmat[(m_ + 16) % 32, m_] = 1.0
    tk = np.arange(128)[:, None]
    tq = np.arange(512)[None, :]
    masks4 = np.concatenate([(tq - tk >= off).astype(np.float32) for off in (0, 128, 256, 384)], 1)
    trilT = (np.arange(128)[:, None] <= np.arange(128)[None, :]).astype(np.float32)
    lam4 = np.stack([np.asarray(inp[k], np.float32).reshape(128) for k in
                     ("lambda_q1", "lambda_k1", "lambda_q2", "lambda_k2")], 0)
    wsT = np.ascontiguousarray(np.asarray(inp["w_spatial"], np.float32)[0].transpose(2, 0, 1)).reshape(128, 8 * 128)
    b1T = np.ascontiguousarray(np.asarray(inp["b_mlp_in"], np.float32).reshape(128, 128).T)
    def blk(w, nkc):
        n = w.shape[1]
        return np.ascontiguousarray(w.reshape(nkc, 128, n // 256, 256).transpose(2, 1, 0, 3)).reshape(n // 256, 128, nkc * 256)

    shared = {
        "w_in": blk(np.asarray(inp["w_in"], np.float32)[0], 32),
        "masks4": masks4, "ident": np.eye(128, dtype=np.float32), "pm": pmat, "trilT": trilT,
        "lam4": lam4, "subln_w": np.asarray(inp["subln_w"], np.float32).reshape(1, DV),
        "sgu_ln_g": np.asarray(inp["sgu_ln_g"], np.float32).reshape(1, SGU_W),
        "sgu_ln_b": np.asarray(inp["sgu_ln_b"], np.float32).reshape(1, SGU_W),
        "wsT": wsT, "b_spatial": np.asarray(inp["b_spatial"], np.float32).reshape(1, 1024),
        "w_proj_attn": blk(np.asarray(inp["w_proj_attn"], np.float32)[0], 16),
        "w_proj_sgu": blk(np.asarray(inp["w_proj_sgu"], np.float32)[0], 16),
        "w_out": blk(np.asarray(inp["w_out"], np.float32)[0], 32),
        "ln1_g": np.asarray(inp["ln1_g"], np.float32).reshape(1, D),
        "ln1_b": np.asarray(inp["ln1_b"], np.float32).reshape(1, D),
        "w_mlp_in": blk(np.asarray(inp["w_mlp_in"], np.float32)[0], 32), "b1T": b1T,
        "w_mlp_out": np.asarray(inp["w_mlp_out"], np.float32)[0],
        "b_mlp_out": np.asarray(inp["b_mlp_out"], np.float32).reshape(1, D),
        "ln2_g": np.asarray(inp["ln2_g"], np.float32).reshape(1, D),
        "ln2_b": np.asarray(inp["ln2_b"], np.float32).reshape(1, D),
    }
    maps = []
    for c in range(NCORES):
        own, order = _core_order(c)
        mb = np.zeros((128, 128), np.float32)
        mb[:, 8 + 4 * c:36] = NEG
        mb[:, 4:8] = NEG
        mb[:, 36:64] = NEG
        mb[:, 64 + 36 + 4 * c:128] = NEG
        m = dict(shared)
        m["xT"] = np.ascontiguousarray(x[order].reshape(16, 512, 32, 128).transpose(0, 3, 2, 1)).reshape(16, 128, 32 * 512)
        m["x_own"] = np.ascontiguousarray(x[own])
        m["ropeC"] = np.ascontiguousarray(C_full[:, order])
        m["ropeS"] = np.ascontiguousarray(S_full[:, order])
        m["maskbias"] = mb
        maps.append(m)
    return maps


def _core_order(c):
    hb = S // 2
    blk = hb // NCORES
    early = np.arange(c * blk, (c + 1) * blk)
    late = np.arange(hb + c * blk, hb + (c + 1) * blk)
    fh = np.concatenate([np.arange(0, c * blk), np.arange((c + 1) * blk, hb)])
    sh = np.concatenate([np.arange(hb, hb + c * blk), np.arange(hb + (c + 1) * blk, S)])
    own = np.concatenate([early, late])
    return own, np.concatenate([own, fh, sh])


_NC_CACHE = {}


def kernel(**inputs):
    maps = _host_inputs(inputs)
    if "nc" not in _NC_CACHE:
        _NC_CACHE["nc"] = build_program()
    nc = _NC_CACHE["nc"]
    names = set(nc_input_names(nc))
    maps = [{k: v for k, v in m.items() if k in names} for m in maps]
    res = run_bass_kernel_spmd(nc, maps, core_ids=list(range(NCORES)))
    full = np.empty((S, D), np.float32)
    for c in range(NCORES):
        own, _ = _core_order(c)
        full[own] = np.asarray(res.results[c]["out"], np.float32)
    return full.reshape(1, S, D)


def nc_input_names(nc):
    names = []
    for alloc in nc.allocations:
        if isinstance(alloc, mybir.MemoryLocationSet) and alloc.kind == "ExternalInput":
            names.append(alloc.memorylocations[0].name)
    return names
```

```python
import math
from contextlib import ExitStack

import numpy as np
import concourse.bass as bass
import concourse.mybir as mybir
from concourse.bass_utils import run_bass_kernel_spmd

F32 = mybir.dt.float32
BF16 = mybir.dt.bfloat16
AF = mybir.ActivationFunctionType
ALU = mybir.AluOpType

NCORES = 8
D = 4096
S = 8192
TOK = S // NCORES
NH = 8
DH = 128
DV = 256
SGU_W = 2048
DFF = 16384
ALPHA = 2.0 ** 0.25
LN_EPS = 1e-5
LAMBDA_INIT = 0.2
ROPE_THETA = 500000.0
NEG = -30000.0
SBUF_BYTES = 207 * 1024
import os
NT_DBG = int(os.environ.get('NT_DBG', '16'))
DO_ROPE = int(os.environ.get('DO_ROPE', '1'))
DO_V = int(os.environ.get('DO_V', '1'))


class Buf:
    def __init__(self, name, excl=False):
        self.name = name
        self.excl = excl
        self.lastw = None
        self.readers = []
        self.dsem = None
        self.dcnt = 0


class Sched:
    ENGS = ("pe", "act", "dve", "pool", "sp")

    def __init__(self, nc, stack):
        self.nc = nc
        self.stack = stack
        self.ops = {e: [] for e in self.ENGS}
        self.cnt = {e: 0 for e in self.ENGS}
        self.sem = {e: stack.enter_context(nc.semaphore("s_" + e)) for e in self.ENGS}
        self.waited = {e: {} for e in self.ENGS}
        self.nsem = 0

    def new_dsem(self, name):
        self.nsem += 1
        return self.stack.enter_context(self.nc.semaphore("d%d_%s" % (self.nsem, name)))

    def _wait(self, eng, tok):
        _, sem, val = tok
        key = id(sem)
        if self.waited[eng].get(key, 0) >= val:
            return
        self.waited[eng][key] = val
        self.ops[eng].append(lambda e, sem=sem, val=val: e.wait_ge(sem, val))

    def _deps(self, eng, reads, writes, same_eng_raw):
        for b in reads:
            w = b.lastw
            if w is not None and (w[0] != eng or same_eng_raw):
                self._wait(eng, w)
            if b.excl:
                for r in b.readers:
                    if r[0] != eng:
                        self._wait(eng, r)
        for b in writes:
            for r in b.readers:
                if r[0] != eng:
                    self._wait(eng, r)
            w = b.lastw
            if w is not None and w[0] != eng:
                self._wait(eng, w)

    def op(self, eng, fn, reads=(), writes=(), sig=True):
        self._deps(eng, reads, writes, same_eng_raw=(eng != "pe"))
        sem = self.sem[eng]
        if sig:
            self.cnt[eng] += 1
            tok = (eng, sem, self.cnt[eng])
            self.ops[eng].append(lambda e, fn=fn, sem=sem: fn(e).then_inc(sem, 1))
        else:
            tok = (eng, sem, self.cnt[eng] + 1)
            self.ops[eng].append(lambda e, fn=fn: fn(e))
        for b in reads:
            b.readers.append(tok)
        for b in writes:
            b.lastw = tok
            b.readers = []

    def dma(self, q, out, in_, reads=(), writes=(), track=None):
        self._deps(q, reads, writes, same_eng_raw=True)
        tb = track if track is not None else (writes[0] if writes else reads[0])
        if tb.dsem is None:
            tb.dsem = self.new_dsem(tb.name)
        tb.dcnt += 16
        tok = ("dma", tb.dsem, tb.dcnt)
        sem = tb.dsem
        self.ops[q].append(lambda e, out=out, in_=in_, sem=sem: e.dma_start(out=out, in_=in_).then_inc(sem, 16))
        for b in reads:
            b.readers.append(tok)
        for b in writes:
            b.lastw = tok
            b.readers = []
        return tok

    def barrier(self, bufs):
        toks = [(e, self.sem[e], self.cnt[e]) for e in ("pe", "act", "dve", "pool") if self.cnt[e] > 0]
        for b in bufs:
            if b.lastw is not None and b.lastw[0] == "dma":
                toks.append(b.lastw)
            for r in b.readers:
                if r[0] == "dma":
                    toks.append(r)
        for e in self.ENGS:
            for t in toks:
                if t[0] != e:
                    self._wait(e, t)

    def wait_tok(self, eng, tok):
        self._wait(eng, tok)


def build_program(stage=99):
    nc = bass.Bass("TRN2", target_bir_lowering=False)

    def din(name, shape, dt=F32):
        return nc.dram_tensor(name, list(shape), dt, kind="ExternalInput").ap()

    xT = din("xT", [16, 128, 32 * 512])
    x_own = din("x_own", [TOK, D]) if stage >= 3 else None
    w_in = din("w_in", [72, 128, 32 * 256])
    ropeC = din("ropeC", [32, S])
    ropeS = din("ropeS", [32, S])
    maskbias = din("maskbias", [128, 128])
    masks4 = din("masks4", [128, 4 * 512])
    ident_in = din("ident", [128, 128])
    pm_in = din("pm", [128, 128])
    tril_in = din("trilT", [128, 128])
    lam_in = din("lam4", [4, 128])
    subln_in = din("subln_w", [1, DV])
    sgu_g_in = din("sgu_ln_g", [1, SGU_W])
    sgu_b_in = din("sgu_ln_b", [1, SGU_W])
    wsT_in = din("wsT", [128, 8 * 128])
    bS_in = din("b_spatial", [1, 8 * 128])
    w_pa = din("w_proj_attn", [16, 128, 16 * 256]) if stage >= 3 else None
    w_ps = din("w_proj_sgu", [16, 128, 16 * 256]) if stage >= 3 else None
    w_out = din("w_out", [16, 128, 32 * 256]) if stage >= 3 else None
    ln1_g = din("ln1_g", [1, D])
    ln1_b = din("ln1_b", [1, D])
    w1 = din("w_mlp_in", [64, 128, 32 * 256]) if stage >= 3 else None
    b1T_in = din("b1T", [128, 128])
    w2 = din("w_mlp_out", [DFF, D]) if stage >= 3 else None
    b2_in = din("b_mlp_out", [1, D])
    ln2_g = din("ln2_g", [1, D])
    ln2_b = din("ln2_b", [1, D])
    out = nc.dram_tensor("out", [TOK, D], F32, kind="ExternalOutput").ap()
    dbg = nc.dram_tensor("dbg", [TOK, 2048], F32, kind="ExternalOutput").ap() if stage < 99 else None
    yaT_d = nc.dram_tensor("yaT_d", [2048, TOK], BF16)

    with ExitStack() as stack:
        SB = stack.enter_context(nc.sbuf_tensor("sb", [128, SBUF_BYTES // 2], BF16))
        banks = [stack.enter_context(nc.psum_tensor("ps%d" % i, [128, 512], F32)) for i in range(8)]
        pbuf = [Buf("ps%d" % i, excl=True) for i in range(8)]
        sc = Sched(nc, stack)

        def carve(off, free_shape, dt):
            esz = 2 if dt == BF16 else 4
            n = 1
            for s_ in free_shape:
                n *= s_
            nbytes = n * esz
            assert off % 4 == 0 and off + nbytes <= SBUF_BYTES, (off, nbytes)
            ap = SB[:, off // 2:(off + nbytes) // 2]
            if dt != BF16:
                ap = ap.bitcast(dt)
            if len(free_shape) == 2:
                ap = ap.rearrange("p (a b) -> p a b", b=free_shape[1])
            elif len(free_shape) == 3:
                ap = ap.rearrange("p (a b c) -> p a b c", b=free_shape[1], c=free_shape[2])
            return ap

        CB = SBUF_BYTES - 8448
        o = CB
        ident = carve(o, [128], BF16); o += 256
        pm = carve(o, [128], BF16); o += 256
        mbias = carve(o, [128], F32); o += 512
        m4 = carve(o, [4, 512], BF16); o += 4096
        lam4 = carve(o, [4, 128], F32); o += 2048
        sublnw = carve(o, [DV], F32); o += 1024
        lamt = carve(o, [8], F32); o += 32
        epsc = carve(o, [1], F32); o += 4
        assert o <= SBUF_BYTES
        consts = Buf("consts")
        sc.dma("pool", ident, ident_in, writes=[consts])
        sc.dma("pool", pm, pm_in, writes=[consts])
        sc.dma("sp", mbias, maskbias, writes=[consts])
        sc.dma("pool", m4, masks4.rearrange("p (a b) -> p a b", b=512), writes=[consts])
        sc.dma("sp", lam4, lam_in.rearrange("a b -> (a b)").partition_broadcast(128).rearrange("p (a b) -> p a b", b=128),
               writes=[consts])
        sc.dma("sp", sublnw, subln_in.rearrange("a b -> (a b)").partition_broadcast(128), writes=[consts])

        sc.op("dve", lambda e: e.memset(epsc, LN_EPS), writes=[consts])
        lamb = Buf("lamb")
        junk128 = carve(CB - 512, [128], F32)
        jb = Buf("junk128")
        sc.op("dve", lambda e: e.tensor_tensor(out=junk128, in0=lam4[:, 0, :], in1=lam4[:, 1, :], op=ALU.mult),
              reads=[consts], writes=[jb])
        sc.op("dve", lambda e: e.tensor_reduce(out=lamt[:, 0:1], in_=junk128, axis=mybir.AxisListType.X, op=ALU.add),
              reads=[jb], writes=[lamb])
        sc.op("dve", lambda e: e.tensor_tensor(out=junk128, in0=lam4[:, 2, :], in1=lam4[:, 3, :], op=ALU.mult),
              reads=[consts, lamb], writes=[jb])
        sc.op("dve", lambda e: e.tensor_reduce(out=lamt[:, 1:2], in_=junk128, axis=mybir.AxisListType.X, op=ALU.add),
              reads=[jb], writes=[lamb])
        sc.op("act", lambda e: e.activation(out=lamt[:, 2:4], in_=lamt[:, 0:2], func=AF.Exp), reads=[lamb], writes=[lamb])
        sc.op("dve", lambda e: e.tensor_tensor(out=lamt[:, 4:5], in0=lamt[:, 2:3], in1=lamt[:, 3:4], op=ALU.subtract),
              reads=[lamb], writes=[lamb])
        sc.op("dve", lambda e: e.tensor_scalar(out=lamt[:, 5:6], in0=lamt[:, 4:5], scalar1=-1.0, scalar2=-LAMBDA_INIT,
                                               op0=ALU.mult, op1=ALU.add), reads=[lamb], writes=[lamb])

        def finish_dbg(src_ap, nrow, ncol, srcbuf):
            t_ = sc.dma("sp", dbg[0:nrow, 0:ncol], src_ap, reads=[srcbuf], writes=[], track=srcbuf)
            sc.wait_tok("sp", t_)
            _emit(nc, sc)
            return nc

        if stage == -2:
            return finish_dbg(lamt, 128, 8, lamb)

        o = 0
        XT = [carve(o, [32, 512], BF16), carve(o + 32768, [32, 512], BF16)]; o += 65536
        Wh = carve(o, [32, 768], BF16); o += 49152
        KT = carve(o, [2, S], BF16); o += 32768
        V = carve(o, [64, 257], BF16); o += 64 * 257 * 2
        QT = carve(o, [2, TOK], BF16); o += 4096
        PT = [carve(o + i * 1024, [512], BF16) for i in range(3)]; o += 3072
        RC = carve(o, [512], F32); o += 2048
        RS = carve(o, [512], F32); o += 2048
        rt1 = carve(o, [512], F32); o += 2048
        rt2 = carve(o, [512], F32); o += 2048
        oacc = carve(o, [4, DV], F32); o += 4096
        ytile4 = carve(o, [4, DV], BF16); o += 2048
        ytT2 = [carve(o, [2, 128], BF16), carve(o + 512, [2, 128], BF16)]; o += 1024
        stt = carve(o, [16], F32); o += 64
        st3 = carve(o, [16], F32); o += 64
        bst3 = Buf("st3")
        assert o <= CB - 512, o
        bXT = [Buf("XT0"), Buf("XT1")]
        bWh, bKT, bV, bQT = Buf("Wh"), Buf("KT"), Buf("V"), Buf("QT")
        bPT = [Buf("PT%d" % i) for i in range(3)]
        bRope, br32, brt1, brt2 = Buf("rope"), Buf("r32b"), Buf("rt1"), Buf("rt2")
        boacc, bstt = Buf("oacc"), Buf("stt")
        bytT2 = [Buf("ytT0"), Buf("ytT1")]
        bytT = bytT2[0]
        bytile4 = [Buf("ytile%d" % i) for i in range(4)]
        deferred = []
        deferred_early = []

        def flush_early():
            for f_ in deferred_early:
                f_()
            del deferred_early[:]

        def flush_deferred():
            flush_early()
            for f_ in deferred:
                f_()
            del deferred[:]
        byaT = Buf("yaT_d")
        scale = DH ** -0.5

        sc.op("dve", lambda e: e.memset(V[:, :, 256:257], 1.0), writes=[bV])

        xt_i = [0]

        def rope_evac(ps_i, dst, dstbuf, t):
            ps = banks[ps_i]
            sc.op("act", lambda e: e.activation(out=dst, in_=ps[:, :], func=AF.Copy), reads=[pbuf[ps_i]], writes=[dstbuf])
            sc.op("pe", lambda e: e.matmul(out=banks[2][:, :], lhsT=pm, rhs=dst, start=True, stop=True),
                  reads=[dstbuf, consts], writes=[pbuf[2]])
            sc.op("dve", lambda e: e.tensor_tensor(out=rt1[0:32, :], in0=ps[0:32, :], in1=RC[0:32, :], op=ALU.mult),
                  reads=[pbuf[ps_i], bRope], writes=[brt1])
            sc.op("dve", lambda e: e.tensor_tensor(out=rt2[0:32, :], in0=banks[2][0:32, :], in1=RS[0:32, :], op=ALU.mult),
                  reads=[pbuf[2], bRope], writes=[brt2])
            sc.op("dve", lambda e: e.tensor_tensor(out=dst[0:32, :], in0=rt1[0:32, :], in1=rt2[0:32, :], op=ALU.add),
                  reads=[brt1, brt2], writes=[dstbuf])

        nheads = NH if stage >= 2 else 1
        for h in range(nheads):
            for part, c0 in enumerate((h * 256, 2048 + h * 256, 4096 + h * 256)):
                sc.dma("pool", Wh[:, :, part * 256:(part + 1) * 256],
                       w_in[c0 // 256].rearrange("p (kc c) -> p kc c", c=256), writes=[bWh])
            for t in range(16):
                slot = xt_i[0] % 2
                xt_i[0] += 1
                sc.dma("pool", XT[slot], xT[t].rearrange("p (kc t) -> p kc t", t=512), writes=[bXT[slot]])
                sc.dma("sp", RC[0:32, :], ropeC[:, t * 512:(t + 1) * 512], writes=[bRope])
                sc.dma("sp", RS[0:32, :], ropeS[:, t * 512:(t + 1) * 512], writes=[bRope])
                if stage == -3 and t >= NT_DBG:
                    break
                groups = [("k", 0), ("k", 1)] + ([("q", 0), ("q", 1)] if t < 2 else [])
                for gi, (kind, m) in enumerate(groups):
                    pi = gi % 2
                    c0 = (256 if kind == "k" else 0) + m * 128
                    for kc in range(32):
                        sc.op("pe", lambda e, pi=pi, kc=kc, c0=c0, slot=slot: e.matmul(
                            out=banks[pi][:, :], lhsT=Wh[:, kc, c0:c0 + 128], rhs=XT[slot][:, kc, :],
                            start=(kc == 0), stop=(kc == 31)),
                            reads=[bWh, bXT[slot]], writes=[pbuf[pi]], sig=(kc == 31))
                    if stage == -3 and not DO_ROPE:
                        dst_ = (KT if kind == "k" else QT)[:, m, t * 512:(t + 1) * 512]
                        sc.op("act", lambda e, pi=pi, dst_=dst_: e.activation(out=dst_, in_=banks[pi][:, :], func=AF.Copy),
                              reads=[pbuf[pi]], writes=[bKT if kind == "k" else bQT])
                    elif kind == "k":
                        rope_evac(pi, KT[:, m, t * 512:(t + 1) * 512], bKT, t)
                    else:
                        rope_evac(pi, QT[:, m, t * 512:(t + 1) * 512], bQT, t)
                if t == 0:
                    flush_early()
                if t == 1:
                    flush_deferred()
                for sub in range(4):
                    if stage == -3 and not DO_V:
                        break
                    pi = sub % 2
                    for kc in range(32):
                        sc.op("pe", lambda e, pi=pi, kc=kc, sub=sub, slot=slot: e.matmul(
                            out=banks[pi][:, 0:256], lhsT=XT[slot][:, kc, sub * 128:(sub + 1) * 128],
                            rhs=Wh[:, kc, 512:768], start=(kc == 0), stop=(kc == 31)),
                            reads=[bWh, bXT[slot]], writes=[pbuf[pi]], sig=(kc == 31))
                    j = t * 4 + sub
                    sc.op("act", lambda e, pi=pi, j=j: e.activation(out=V[:, j, 0:256], in_=banks[pi][:, 0:256], func=AF.Copy),
                          reads=[pbuf[pi]], writes=[bV])

            if stage in (-1, -3):
                kt32 = carve(0, [TOK], F32)
                bk = Buf("kt32")
                sc.op("dve", lambda e: e.tensor_copy(out=kt32, in_=KT[:, 0, 0:TOK]), reads=[bKT, bXT[0], bXT[1]], writes=[bk])
                return finish_dbg(kt32, 128, TOK, bk)
            for qh in range(2):
                for m in range(2):
                    if qh == 0:
                        jlist = [(j, j * 128) for j in range(4)] + [(j, None) for j in range(8, 36)]
                    else:
                        jlist = ([(j, None) for j in range(4)] + [(j, (j - 4) * 128) for j in range(4, 8)]
                                 + [(j, None) for j in range(8, 64)])
                    n_t = len(jlist)

                    def emit_s(idx, m=m, qh=qh, jlist=jlist):
                        j, off = jlist[idx]
                        sb_i = idx % 4
                        p_i = idx % 3
                        sc.op("pe", lambda e: e.matmul(
                            out=banks[sb_i][:, :], lhsT=KT[:, m, j * 128:(j + 1) * 128],
                            rhs=QT[:, m, qh * 512:(qh + 1) * 512], start=True, stop=True),
                            reads=[bKT, bQT], writes=[pbuf[sb_i]])
                        sc.op("act", lambda e: e.activation(
                            out=PT[p_i], in_=banks[sb_i][:, :], func=AF.Exp, bias=mbias[:, qh * 64 + j:qh * 64 + j + 1], scale=scale),
                            reads=[pbuf[sb_i], consts], writes=[bPT[p_i]])
                        if off is not None:
                            mi = off // 128
                            sc.op("dve", lambda e: e.tensor_tensor(
                                out=PT[p_i], in0=PT[p_i], in1=m4[:, mi, :], op=ALU.mult),
                                reads=[bPT[p_i], consts], writes=[bPT[p_i]])

                    def emit_pv(idx, jlist=jlist, n_t=n_t):
                        j, off = jlist[idx]
                        p_i = idx % 3
                        for qs in range(4):
                            if off is not None and qs * 128 + 127 < off:
                                if idx != 0 and idx != n_t - 1:
                                    continue
                            sc.op("pe", lambda e, qs=qs: e.matmul(
                                out=banks[4 + qs][:, 0:257], lhsT=PT[p_i][:, qs * 128:(qs + 1) * 128],
                                rhs=V[:, j, :], start=(idx == 0), stop=(idx == n_t - 1)),
                                reads=[bPT[p_i], bV], writes=[pbuf[4 + qs]], sig=True)

                    LA = 2
                    for idx in range(n_t + LA):
                        if idx < n_t:
                            emit_s(idx)
                        if idx - LA >= 0:
                            emit_pv(idx - LA)
                        if idx == 16:
                            flush_early()
                        if idx == 40:
                            flush_deferred()
                    flush_deferred()
                    for qs in range(4):
                        ob = banks[4 + qs]
                        sc.op("dve", lambda e, ob=ob, m=m: e.reciprocal(out=stt[:, m:m + 1], in_=ob[:, 256:257]),
                              reads=[pbuf[4 + qs]], writes=[bstt])
                        if m == 0:
                            sc.op("dve", lambda e, ob=ob, qs=qs: e.tensor_scalar(
                                out=oacc[:, qs, :], in0=ob[:, 0:256], scalar1=stt[:, 0:1], scalar2=None, op0=ALU.mult),
                                reads=[pbuf[4 + qs], bstt], writes=[boacc])
                        else:
                            sc.op("dve", lambda e: e.tensor_tensor(out=stt[:, 2:3], in0=stt[:, 1:2], in1=lamt[:, 5:6], op=ALU.mult),
                                  reads=[bstt, lamb], writes=[bstt])
                            sc.op("dve", lambda e, ob=ob, qs=qs: e.scalar_tensor_tensor(
                                out=oacc[:, qs, :], in0=ob[:, 0:256], scalar=stt[:, 2:3], in1=oacc[:, qs, :],
                                op0=ALU.mult, op1=ALU.add), reads=[pbuf[4 + qs], bstt, boacc], writes=[boacc])
                    if m == 1:
                        for qs in range(4):
                            sc.op("dve", lambda e, qs=qs: e.bn_stats(out=stt[:, 4:10], in_=oacc[:, qs, :]),
                                  reads=[boacc], writes=[bstt])
                            sc.op("dve", lambda e: e.bn_aggr(out=stt[:, 10:12], in_=stt[:, 4:10]), reads=[bstt], writes=[bstt])
                            sc.op("dve", lambda e: e.tensor_tensor(out=stt[:, 12:13], in0=stt[:, 10:11], in1=stt[:, 10:11], op=ALU.mult),
                                  reads=[bstt], writes=[bstt])
                            sc.op("dve", lambda e, qs=qs: e.tensor_tensor(out=st3[:, qs:qs + 1], in0=stt[:, 12:13], in1=stt[:, 11:12], op=ALU.add),
                                  reads=[bstt], writes=[bst3])

                        def norm_tail():
                            sc.op("act", lambda e: e.activation(out=st3[:, 4:8], in_=st3[:, 0:4], func=AF.Sqrt, bias=epsc[:, 0:1], scale=1.0),
                                  reads=[bst3, consts], writes=[bst3])
                            sc.op("dve", lambda e: e.reciprocal(out=st3[:, 8:12], in_=st3[:, 4:8]), reads=[bst3], writes=[bst3])
                            for qs_ in range(4):
                                sc.op("dve", lambda e, qs_=qs_: e.scalar_tensor_tensor(
                                    out=oacc[:, qs_, :], in0=oacc[:, qs_, :], scalar=st3[:, 8 + qs_:9 + qs_], in1=sublnw,
                                    op0=ALU.mult, op1=ALU.mult), reads=[boacc, bst3, consts], writes=[boacc])
                                sc.op("dve", lambda e, qs_=qs_: e.tensor_scalar(
                                    out=ytile4[:, qs_, :], in0=oacc[:, qs_, :], scalar1=1.0 - LAMBDA_INIT, scalar2=None, op0=ALU.mult),
                                    reads=[boacc], writes=[bytile4[qs_]])
                        deferred_early.append(norm_tail)
                        for qs in range(4):

                            def tail(qs=qs, qh=qh, h=h):
                                tbk = 2 + (qs % 2)
                                tb = banks[tbk].bitcast(BF16)
                                ytT = ytT2[qs % 2]
                                for hh in range(2):
                                    sc.op("pe", lambda e, hh=hh: e.transpose(
                                        out=tb[:, hh * 128:(hh + 1) * 128], in_=ytile4[:, qs, hh * 128:(hh + 1) * 128], identity=ident),
                                        reads=[bytile4[qs], consts], writes=[pbuf[tbk]])
                                sc.op("act", lambda e: e.activation(out=ytT, in_=tb[:, 0:256].rearrange("p (a b) -> p a b", b=128),
                                                                    func=AF.Copy), reads=[pbuf[tbk]], writes=[bytT2[qs % 2]])
                                tok0 = qh * 512 + qs * 128
                                sc.dma("sp", yaT_d[h * 256:(h + 1) * 256, tok0:tok0 + 128].rearrange("(a p) t -> p a t", p=128),
                                       ytT, reads=[bytT2[qs % 2]], writes=[byaT], track=byaT)
                            deferred.append(tail)
        flush_deferred()

        if stage < 3:
            dbt = carve(0, [TOK], F32)
            dbb = carve(8192, [TOK], BF16)
            bdb = Buf("dbg")
            sc.barrier([byaT, bXT[0], bXT[1], bWh])
            for c in range(2 * nheads):
                sc.dma("sp", dbb, yaT_d[c * 128:(c + 1) * 128, :], reads=[byaT], writes=[bdb])
                sc.op("dve", lambda e: e.tensor_copy(out=dbt, in_=dbb), reads=[bdb], writes=[bdb])
                t_ = sc.dma("sp", dbg[:, c * 128:(c + 1) * 128].rearrange("(a p) f -> p a f", p=128) if False else
                            dbg[c * 128:(c + 1) * 128, 0:TOK], dbt, reads=[bdb], writes=[], track=bdb)
            sc.wait_tok("sp", t_)
            _emit(nc, sc)
            return nc

        sc.barrier([byaT, bXT[0], bXT[1], bWh, bRope, bytT2[0], bytT2[1], bKT, bV, bQT])
        bR = [Buf("R%d" % i) for i in range(4)]
        bM = [Buf("M0"), Buf("M1")]
        bX = [Buf("X0"), Buf("X1")]
        o = 0
        R = carve(0, [4, D], F32); o += 65536
        ug = carve(0, [16, 512], BF16)
        sln = carve(16384, [4, SGU_W], BF16)
        ybT = carve(32768, [16, 512], BF16)
        yaT = carve(49152, [16, 512], BF16)
        sg4 = carve(32768, [4, SGU_W], F32)
        MO = o; o += 32768
        sgug = carve(MO + 16384, [SGU_W], F32)
        sgub = carve(MO + 24576, [SGU_W], F32)
        mT = carve(MO, [32, 512], BF16)
        XO = o; o += 32768
        XTo = carve(XO, [32, 512], BF16)
        PB = [carve(XO, [D], F32), carve(XO + 16384, [D], F32)]
        PB2 = [carve(MO, [D], F32), carve(MO + 16384, [D], F32)]
        ring_offs = [o, o + 16384, o + 32768, XO, XO + 16384]
        ring = [carve(off_, [32, 256], BF16) for off_ in ring_offs]
        ringW2 = [carve(off_, [2, D], BF16) for off_ in ring_offs]
        o += 49152
        bring = [Buf("ring0"), Buf("ring1"), Buf("ring2"), bX[0], bX[1]]
        zT = [carve(o, [4, 512], BF16), carve(o + 4096, [4, 512], BF16)]; o += 8192
        bzT = [Buf("zT0"), Buf("zT1")]
        bS = carve(o, [8, 128], F32); o += 4096
        wsT = carve(o, [8, 128], BF16); o += 2048
        tril = carve(o, [128], BF16); o += 256
        b1T = carve(o, [128], F32); o += 512
        identf = carve(o, [128], F32); o += 512
        gt_off = o
        gt = [carve(o + i * 2048, [512], F32) for i in range(3)]; o += 6144
        bgt = Buf("gt")
        st = carve(o, [64], F32); o += 256
        bst = Buf("st")
        sgm = [carve(o - 256 - 6144 - 512 - 512 - 256 - 2048 - 4096 - 8192, [2, 512], BF16),
               carve(o - 256 - 6144 - 512 - 512 - 256 - 2048 - 4096 - 8192 + 2048, [2, 512], BF16)]
        bsgm = [bzT[0], bzT[0]]
        p1 = carve(gt_off + 2048, [2, 512], F32)
        bp1 = bgt
        p2 = gt[0]
        bp2 = bgt
        assert o <= CB - 512, o
        cB = Buf("constsB")
        sc.dma("pool", wsT, wsT_in.rearrange("p (g t) -> p g t", t=128), writes=[cB])
        sc.dma("pool", tril, tril_in, writes=[cB])
        sc.dma("sp", bS, bS_in.rearrange("a b -> (a b)").partition_broadcast(128).rearrange("p (g t) -> p g t", t=128), writes=[cB])
        sc.dma("sp", b1T, b1T_in, writes=[cB])
        sc.dma("sp", identf, ident_in, writes=[cB])
        for g in range(8):
            sc.op("dve", lambda e, g=g: e.tensor_tensor(out=wsT[:, g, :], in0=wsT[:, g, :], in1=tril, op=ALU.mult),
                  reads=[cB], writes=[cB])

        bank_i = [0]

        def nb():
            bank_i[0] = (bank_i[0] + 1) % 8
            return bank_i[0]

        blocks = []
        for th in range(2):
            for b in range(8):
                blocks.append((w_in[32 + b], 32, 0))
            for b in range(8):
                blocks.append((w_in[24 + b], 32, 0))
            for cb in range(16):
                blocks.append((w_in[40 + cb], 32, 0))
                blocks.append((w_pa[cb], 16, 0))
                blocks.append((w_in[56 + cb], 32, 0))
                blocks.append((w_ps[cb], 16, 0))
            for cb in range(16):
                blocks.append((w_out[cb], 32, 0))
            def _a(pr):
                for ffb in (2 * pr, 2 * pr + 1):
                    blocks.append((w1[ffb], 32, 0, 1))

            def _b(pr):
                for ffb in (2 * pr, 2 * pr + 1):
                    blocks.append((w2[ffb * 256:(ffb + 1) * 256, :], 2, 1, 1))
            _a(0)
            for pr in range(32):
                if pr + 1 < 32:
                    _a(pr + 1)
                _b(pr)
        last_use = [-10 - i for i in range(5)]
        slot_of = []
        mlp_run = 0
        for k_, blk_ in enumerate(blocks):
            mlp_run = mlp_run + 1 if len(blk_) > 3 else 0
            allowed = range(5) if mlp_run > 2 else range(3)
            s_ = min(allowed, key=lambda a: last_use[a])
            assert k_ - last_use[s_] >= 3
            last_use[s_] = k_
            slot_of.append(s_)
        issued = [0]
        used = [0]

        def next_block():
            i = used[0]
            used[0] += 1
            while issued[0] < min(i + 3, len(blocks)):
                k_ = issued[0]
                src, nk, kind = blocks[k_][:3]
                sl = slot_of[k_]
                if kind == 0:
                    sc.dma("pool", ring[sl][:, 0:nk, :], src.rearrange("p (kc c) -> p kc c", c=256), writes=[bring[sl]])
                else:
                    sc.dma("pool", ringW2[sl], src.rearrange("(fc p) d -> p fc d", p=128), writes=[bring[sl]])
                issued[0] += 1
            return slot_of[i]

        def fm_group(sl, nk, c, rhsT, rbufs, bank):
            for kc in range(nk):
                sc.op("pe", lambda e, kc=kc: e.matmul(out=banks[bank][:, :], lhsT=ring[sl][:, kc, c * 128:(c + 1) * 128],
                                                      rhs=rhsT[:, kc, :], start=(kc == 0), stop=(kc == nk - 1)),
                      reads=[bring[sl]] + rbufs, writes=[pbuf[bank]], sig=(kc == nk - 1))

        def tm_group(sl, nk, lsrc, lbufs, tt, bank):
            for kc in range(nk):
                sc.op("pe", lambda e, kc=kc: e.matmul(out=banks[bank][:, 0:256], lhsT=lsrc[:, kc, tt * 128:(tt + 1) * 128],
                                                      rhs=ring[sl][:, kc, :], start=(kc == 0), stop=(kc == nk - 1)),
                      reads=[bring[sl]] + lbufs, writes=[pbuf[bank]], sig=(kc == nk - 1))

        def gelu(dst, dbufs, bank, n):
            ps = banks[bank][:, 0:n]
            sc.op("act", lambda e: e.activation(out=gt[0][:, 0:n], in_=ps, func=AF.Copy), reads=[pbuf[bank]], writes=[bgt])
            sc.op("dve", lambda e: e.tensor_tensor(out=gt[1][:, 0:n], in0=gt[0][:, 0:n], in1=gt[0][:, 0:n], op=ALU.mult),
                  reads=[bgt], writes=[bgt])
            sc.op("dve", lambda e: e.tensor_scalar(out=gt[1][:, 0:n], in0=gt[1][:, 0:n], scalar1=0.044715, scalar2=1.0,
                                                   op0=ALU.mult, op1=ALU.add), reads=[bgt], writes=[bgt])
            sc.op("dve", lambda e: e.tensor_tensor(out=gt[1][:, 0:n], in0=gt[1][:, 0:n], in1=gt[0][:, 0:n], op=ALU.mult),
                  reads=[bgt], writes=[bgt])
            sc.op("act", lambda e: e.activation(out=gt[2][:, 0:n], in_=gt[1][:, 0:n], func=AF.Sigmoid, scale=1.5957691216057308),
                  reads=[bgt], writes=[bgt])
            sc.op("dve", lambda e: e.tensor_tensor(out=dst, in0=gt[0][:, 0:n], in1=gt[2][:, 0:n], op=ALU.mult),
                  reads=[bgt], writes=dbufs)

        def layer_norm(xap, xbufs, nfeat, g_ap, b_ap, gbufs, out_ap, obufs):
            nch = nfeat // 512
            for ci in range(nch):
                sc.op("dve", lambda e, ci=ci: e.bn_stats(out=st[:, ci * 6:(ci + 1) * 6], in_=xap[:, ci * 512:(ci + 1) * 512]),
                      reads=xbufs, writes=[bst])
            sc.op("dve", lambda e: e.bn_aggr(out=st[:, 48:50], in_=st[:, 0:nch * 6]), reads=[bst], writes=[bst])
            sc.op("act", lambda e: e.activation(out=st[:, 50:51], in_=st[:, 49:50], func=AF.Sqrt, bias=epsc[:, 0:1], scale=1.0),
                  reads=[bst, consts], writes=[bst])
            sc.op("dve", lambda e: e.reciprocal(out=st[:, 51:52], in_=st[:, 50:51]), reads=[bst], writes=[bst])
            sc.op("dve", lambda e: e.tensor_scalar(out=xap, in0=xap, scalar1=st[:, 48:49], scalar2=st[:, 51:52],
                                                   op0=ALU.subtract, op1=ALU.mult), reads=xbufs + [bst], writes=xbufs)
            sc.op("dve", lambda e: e.tensor_tensor(out=xap, in0=xap, in1=g_ap, op=ALU.mult), reads=xbufs + gbufs, writes=xbufs)
            sc.op("dve", lambda e: e.tensor_tensor(out=out_ap, in0=xap, in1=b_ap, op=ALU.add), reads=xbufs + gbufs, writes=obufs)

        bout = [Buf("out%d" % i) for i in range(4)]
        last_out = [None] * 4
        for th in range(2):
            t0 = th * 512
            sc.dma("pool", XTo, xT[th].rearrange("p (kc t) -> p kc t", t=512), writes=[bX[0], bX[1]])
            sc.dma("sp", sgug, sgu_g_in.rearrange("a b -> (a b)").partition_broadcast(128), writes=[bM[1]])
            sc.dma("sp", sgub, sgu_b_in.rearrange("a b -> (a b)").partition_broadcast(128), writes=[bM[1]])
            for b in range(8):
                sl = next_block()
                for tt in range(4):
                    bk = nb()
                    tm_group(sl, 32, XTo, [bX[0], bX[1]], tt, bk)
                    gelu(sg4[:, tt, b * 256:(b + 1) * 256], [bR[2 + tt // 2]], bk, 256)
            for tt in range(4):
                layer_norm(sg4[:, tt, :], [bR[2 + tt // 2]], SGU_W, sgug, sgub, [bM[1]], sln[:, tt, :], [bR[1]])
            for b in range(8):
                sl = next_block()
                for c in range(2):
                    bk = nb()
                    fm_group(sl, 32, c, XTo, [bX[0], bX[1]], bk)
                    gelu(ug[:, b * 2 + c, :], [bR[0]], bk, 512)
            for g in range(8):
                for dh in range(2):
                    bk = nb()
                    ch = g * 2 + dh
                    for tt in range(4):
                        sc.op("pe", lambda e, tt=tt, g=g, ch=ch, bk=bk: e.matmul(
                            out=banks[bk][:, tt * 128:(tt + 1) * 128], lhsT=sln[:, tt, ch * 128:(ch + 1) * 128],
                            rhs=wsT[:, g, :], start=True, stop=True),
                            reads=[bR[1], cB], writes=[pbuf[bk]], sig=(tt == 3))
                    for tt in range(4):
                        sc.op("dve", lambda e, tt=tt, g=g, bk=bk: e.tensor_tensor(
                            out=p1[:, 0, tt * 128:(tt + 1) * 128], in0=banks[bk][:, tt * 128:(tt + 1) * 128], in1=bS[:, g, :],
                            op=ALU.add), reads=[pbuf[bk], cB], writes=[bp1])
                    sc.op("dve", lambda e, ch=ch: e.tensor_tensor(out=ybT[:, ch, :], in0=p1[:, 0, :], in1=ug[:, ch, :], op=ALU.mult),
                          reads=[bp1, bR[0]], writes=[bR[2]])
            sc.dma("sp", yaT, yaT_d[:, t0:t0 + 512].rearrange("(kc p) t -> p kc t", p=128), reads=[byaT], writes=[bR[3]])
            for cb in range(16):
                sl = next_block()
                for c in range(2):
                    bk = nb()
                    fm_group(sl, 32, c, XTo, [bX[0], bX[1]], bk)
                    sc.op("act", lambda e, c=c, bk=bk: e.activation(out=sgm[0][:, c, :], in_=banks[bk][:, :], func=AF.Sigmoid),
                          reads=[pbuf[bk]], writes=[bsgm[0]])
                sl = next_block()
                for c in range(2):
                    bk = nb()
                    fm_group(sl, 16, c, yaT, [bR[3]], bk)
                    sc.op("dve", lambda e, c=c, bk=bk: e.tensor_tensor(out=p1[:, c, :], in0=banks[bk][:, :], in1=sgm[0][:, c, :], op=ALU.mult),
                          reads=[pbuf[bk], bsgm[0]], writes=[bp1])
                sl = next_block()
                for c in range(2):
                    bk = nb()
                    fm_group(sl, 32, c, XTo, [bX[0], bX[1]], bk)
                    sc.op("act", lambda e, c=c, bk=bk: e.activation(out=sgm[1][:, c, :], in_=banks[bk][:, :], func=AF.Sigmoid),
                          reads=[pbuf[bk]], writes=[bsgm[1]])
                sl = next_block()
                for c in range(2):
                    bk = nb()
                    fm_group(sl, 16, c, ybT, [bR[2]], bk)
                    sc.op("dve", lambda e, c=c, bk=bk: e.tensor_tensor(out=p2, in0=banks[bk][:, :], in1=sgm[1][:, c, :], op=ALU.mult),
                          reads=[pbuf[bk], bsgm[1]], writes=[bp2])
                    sc.op("dve", lambda e, c=c, cb=cb: e.tensor_tensor(out=mT[:, cb * 2 + c, :], in0=p2, in1=p1[:, c, :], op=ALU.add),
                          reads=[bp2, bp1], writes=[bM[0], bM[1]])
            for tt in range(4):
                sc.dma("sp", R[:, tt, :], x_own[t0 + tt * 128:t0 + (tt + 1) * 128, :], writes=[bR[tt]])
            for cb in range(16):
                sl = next_block()
                for tt in range(4):
                    bk = nb()
                    tm_group(sl, 32, mT, [bM[0], bM[1]], tt, bk)
                    sc.op("dve", lambda e, tt=tt, cb=cb, bk=bk: e.scalar_tensor_tensor(
                        out=R[:, tt, cb * 256:(cb + 1) * 256], in0=R[:, tt, cb * 256:(cb + 1) * 256], scalar=ALPHA,
                        in1=banks[bk][:, 0:256], op0=ALU.mult, op1=ALU.add), reads=[pbuf[bk], bR[tt]], writes=[bR[tt]])
            sc.dma("sp", PB[0], ln1_g.rearrange("a b -> (a b)").partition_broadcast(128), writes=[bX[0]])
            sc.dma("sp", PB[1], ln1_b.rearrange("a b -> (a b)").partition_broadcast(128), writes=[bX[1]])
            for tt in range(4):
                layer_norm(R[:, tt, :], [bR[tt]], D, PB[0], PB[1], [bX[0], bX[1]], R[:, tt, :], [bR[tt]])
                for k4 in range(8):
                    bk = nb()
                    for i in range(4):
                        kc = k4 * 4 + i
                        sc.op("pe", lambda e, i=i, kc=kc, tt=tt, bk=bk: e.transpose(
                            out=banks[bk][:, i * 128:(i + 1) * 128], in_=R[:, tt, kc * 128:(kc + 1) * 128], identity=identf),
                            reads=[bR[tt], cB], writes=[pbuf[bk]], sig=(i == 3))
                    sc.op("act", lambda e, k4=k4, tt=tt, bk=bk: e.activation(
                        out=mT[:, k4 * 4:(k4 + 1) * 4, tt * 128:(tt + 1) * 128],
                        in_=banks[bk][:, :].rearrange("p (a b) -> p a b", b=128), func=AF.Copy),
                        reads=[pbuf[bk]], writes=[bM[0], bM[1]])
            sc.dma("sp", PB[0], b2_in.rearrange("a b -> (a b)").partition_broadcast(128), writes=[bX[0]])
            for tt in range(4):
                sc.op("dve", lambda e, tt=tt: e.scalar_tensor_tensor(out=R[:, tt, :], in0=R[:, tt, :], scalar=ALPHA, in1=PB[0],
                                                                    op0=ALU.mult, op1=ALU.add), reads=[bR[tt], bX[0]], writes=[bR[tt]])
            def z_part(pr):
                zi = pr % 2
                a_sl = [next_block(), next_block()]
                for q in range(4):
                    bk = nb()
                    fm_group(a_sl[q // 2], 32, q % 2, mT, [bM[0], bM[1]], bk)
                    fcol = pr * 4 + q
                    sc.op("act", lambda e, bk=bk, fcol=fcol: e.activation(out=gt[0], in_=banks[bk][:, :], func=AF.Relu,
                                                                          bias=b1T[:, fcol:fcol + 1], scale=1.0),
                          reads=[pbuf[bk], cB], writes=[bgt])
                    sc.op("dve", lambda e, zi=zi, q=q: e.tensor_tensor(out=zT[zi][:, q, :], in0=gt[0], in1=gt[0], op=ALU.mult),
                          reads=[bgt], writes=[bzT[zi]])

            def ff_part(pr):
                zi = pr % 2
                b_sl = [next_block(), next_block()]
                for tt in range(4):
                    for db in range(8):
                        bk = nb()
                        for fc in range(4):
                            sl2 = b_sl[fc // 2]
                            sc.op("pe", lambda e, fc=fc, tt=tt, db=db, bk=bk, zi=zi, sl2=sl2: e.matmul(
                                out=banks[bk][:, :], lhsT=zT[zi][:, fc, tt * 128:(tt + 1) * 128],
                                rhs=ringW2[sl2][:, fc % 2, db * 512:(db + 1) * 512], start=(fc == 0), stop=(fc == 3)),
                                reads=[bzT[zi], bring[sl2]], writes=[pbuf[bk]], sig=(fc == 3))
                        sc.op("dve", lambda e, tt=tt, db=db, bk=bk: e.tensor_tensor(
                            out=R[:, tt, db * 512:(db + 1) * 512], in0=R[:, tt, db * 512:(db + 1) * 512], in1=banks[bk][:, :],
                            op=ALU.add), reads=[pbuf[bk], bR[tt]], writes=[bR[tt]])

            z_part(0)
            for pr in range(32):
                if pr + 1 < 32:
                    z_part(pr + 1)
                ff_part(pr)
            sc.dma("sp", PB2[0], ln2_g.rearrange("a b -> (a b)").partition_broadcast(128), writes=[bM[0]])
            sc.dma("sp", PB2[1], ln2_b.rearrange("a b -> (a b)").partition_broadcast(128), writes=[bM[1]])
            for tt in range(4):
                layer_norm(R[:, tt, :], [bR[tt]], D, PB2[0], PB2[1], [bM[0], bM[1]], R[:, tt, :], [bR[tt]])
                r0 = t0 + tt * 128
                last_out[tt] = sc.dma("sp", out[r0:r0 + 128, :], R[:, tt, :], reads=[bR[tt]], writes=[], track=bout[tt])
        for tok_ in last_out:
            sc.wait_tok("sp", tok_)
        _emit(nc, sc)
        return nc


def _emit(nc, sc):
    with nc.Block() as block:
        @block.tensor
        def _(e):
            for f in sc.ops["pe"]:
                f(e)

        @block.scalar
        def _(e):
            for f in sc.ops["act"]:
                f(e)

        @block.vector
        def _(e):
            for f in sc.ops["dve"]:
                f(e)

        @block.gpsimd
        def _(e):
            for f in sc.ops["pool"]:
                f(e)

        @block.sync
        def _(e):
            for f in sc.ops["sp"]:
                f(e)


def _host_inputs(inp):
    x = np.asarray(inp["x"], np.float32).reshape(S, D)
    half = 8
    inv_freq = (ROPE_THETA ** (-np.arange(0, 32, 2, dtype=np.float32) / np.float32(32))).astype(np.float32)
    pos = np.arange(S, dtype=np.float32)
    ang = (pos[:, None] * inv_freq[None, :]).astype(np.float32)
    cosT = np.cos(ang).astype(np.float32).T
    sinT = np.sin(ang).astype(np.float32).T
    C_full = np.concatenate([cosT, cosT], 0)
    S_full = np.concatenate([-sinT, sinT], 0)
    pmat = np.zeros((128, 128), np.float32)
    for m_ in range(32):
        pmat[(m_ + 16) % 32, m_] = 1.0
    tk = np.arange(128)[:, None]
    tq = np.arange(512)[None, :]
    masks4 = np.concatenate([(tq - tk >= off).astype(np.float32) for off in (0, 128, 256, 384)], 1)
    trilT = (np.arange(128)[:, None] <= np.arange(128)[None, :]).astype(np.float32)
    lam4 = np.stack([np.asarray(inp[k], np.float32).reshape(128) for k in
                     ("lambda_q1", "lambda_k1", "lambda_q2", "lambda_k2")], 0)
    wsT = np.ascontiguousarray(np.asarray(inp["w_spatial"], np.float32)[0].transpose(2, 0, 1)).reshape(128, 8 * 128)
    b1T = np.ascontiguousarray(np.asarray(inp["b_mlp_in"], np.float32).reshape(128, 128).T)
    def blk(w, nkc):
        n = w.shape[1]
        return np.ascontiguousarray(w.reshape(nkc, 128, n // 256, 256).transpose(2, 1, 0, 3)).reshape(n // 256, 128, nkc * 256)

    shared = {
        "w_in": blk(np.asarray(inp["w_in"], np.float32)[0], 32),
        "masks4": masks4, "ident": np.eye(128, dtype=np.float32), "pm": pmat, "trilT": trilT,
        "lam4": lam4, "subln_w": np.asarray(inp["subln_w"], np.float32).reshape(1, DV),
        "sgu_ln_g": np.asarray(inp["sgu_ln_g"], np.float32).reshape(1, SGU_W),
        "sgu_ln_b": np.asarray(inp["sgu_ln_b"], np.float32).reshape(1, SGU_W),
        "wsT": wsT, "b_spatial": np.asarray(inp["b_spatial"], np.float32).reshape(1, 1024),
        "w_proj_attn": blk(np.asarray(inp["w_proj_attn"], np.float32)[0], 16),
        "w_proj_sgu": blk(np.asarray(inp["w_proj_sgu"], np.float32)[0], 16),
        "w_out": blk(np.asarray(inp["w_out"], np.float32)[0], 32),
        "ln1_g": np.asarray(inp["ln1_g"], np.float32).reshape(1, D),
        "ln1_b": np.asarray(inp["ln1_b"], np.float32).reshape(1, D),
        "w_mlp_in": blk(np.asarray(inp["w_mlp_in"], np.float32)[0], 32), "b1T": b1T,
        "w_mlp_out": np.asarray(inp["w_mlp_out"], np.float32)[0],
        "b_mlp_out": np.asarray(inp["b_mlp_out"], np.float32).reshape(1, D),
        "ln2_g": np.asarray(inp["ln2_g"], np.float32).reshape(1, D),
        "ln2_b": np.asarray(inp["ln2_b"], np.float32).reshape(1, D),
    }
    maps = []
    for c in range(NCORES):
        own, order = _core_order(c)
        mb = np.zeros((128, 128), np.float32)
        mb[:, 8 + 4 * c:36] = NEG
        mb[:, 4:8] = NEG
        mb[:, 36:64] = NEG
        mb[:, 64 + 36 + 4 * c:128] = NEG
        m = dict(shared)
        m["xT"] = np.ascontiguousarray(x[order].reshape(16, 512, 32, 128).transpose(0, 3, 2, 1)).reshape(16, 128, 32 * 512)
        m["x_own"] = np.ascontiguousarray(x[own])
        m["ropeC"] = np.ascontiguousarray(C_full[:, order])
        m["ropeS"] = np.ascontiguousarray(S_full[:, order])
        m["maskbias"] = mb
        maps.append(m)
    return maps


def _core_order(c):
    hb = S // 2
    blk = hb // NCORES
    early = np.arange(c * blk, (c + 1) * blk)
    late = np.arange(hb + c * blk, hb + (c + 1) * blk)
    fh = np.concatenate([np.arange(0, c * blk), np.arange((c + 1) * blk, hb)])
    sh = np.concatenate([np.arange(hb, hb + c * blk), np.arange(hb + (c + 1) * blk, S)])
    own = np.concatenate([early, late])
    return own, np.concatenate([own, fh, sh])


_NC_CACHE = {}


def kernel(**inputs):
    maps = _host_inputs(inputs)
    if "nc" not in _NC_CACHE:
        _NC_CACHE["nc"] = build_program()
    nc = _NC_CACHE["nc"]
    names = set(nc_input_names(nc))
    maps = [{k: v for k, v in m.items() if k in names} for m in maps]
    res = run_bass_kernel_spmd(nc, maps, core_ids=list(range(NCORES)))
    full = np.empty((S, D), np.float32)
    for c in range(NCORES):
        own, _ = _core_order(c)
        full[own] = np.asarray(res.results[c]["out"], np.float32)
    return full.reshape(1, S, D)


def nc_input_names(nc):
    names = []
    for alloc in nc.allocations:
        if isinstance(alloc, mybir.MemoryLocationSet) and alloc.kind == "ExternalInput":
            names.append(alloc.memorylocations[0].name)
    return names
```
